# Optimizing a Trainium2 kernel written in Bass

```python
import jax
import jax.numpy as jnp
from jax import lax
import numpy as np

D_MODEL = 2048
BATCH = 4
SEQ = 8192
DEPTH = 1

CTX_LEN = 256
GRID_W = 64
HEAD_DIM = 128
ATTN_Q_HEADS = 8
ATTN_KV_HEADS = 2
ATTN_GROUPS = ATTN_Q_HEADS // ATTN_KV_HEADS
ATTN_WIDTH = ATTN_Q_HEADS * HEAD_DIM
KV_WIDTH = ATTN_KV_HEADS * HEAD_DIM
RET_HEADS = 8
RET_QK_DIM = 128
RET_V_DIM = 128
RET_QK_WIDTH = RET_HEADS * RET_QK_DIM
RET_V_WIDTH = RET_HEADS * RET_V_DIM
D_FF = 5632
Q_BLOCK = 128
RET_CHUNK = 128
ROPE_THETA = 10000.0
NORM_EPS = 1e-6
N_MOD = 9
PROJ_SPLITS = (ATTN_WIDTH, KV_WIDTH, KV_WIDTH, RET_QK_WIDTH, RET_QK_WIDTH, RET_V_WIDTH, RET_V_WIDTH, D_MODEL, D_MODEL)
PROJ_WIDTH = ATTN_WIDTH + 2 * KV_WIDTH + 2 * RET_QK_WIDTH + 2 * RET_V_WIDTH + 2 * D_MODEL

kernel_name = "hybrid_gqa_retention_macaron_dit_block"


def _rms(x):
    xf = x.astype(jnp.float32)
    return (xf * lax.rsqrt(jnp.mean(xf * xf, axis=-1, keepdims=True) + NORM_EPS)).astype(x.dtype)


def _modulate(n, shift, scale):
    return n * (1 + scale) + shift


def _adaln(cond, w_ada, b_ada):
    return jnp.split(jax.nn.silu(cond) @ w_ada + b_ada, N_MOD, axis=-1)


def _swiglu(x, w_in, w_out):
    a, b = jnp.split(x @ w_in, 2, axis=-1)
    return (jax.nn.silu(a) * b) @ w_out


def _split_proj(p):
    cuts = []
    acc = 0
    for w in PROJ_SPLITS[:-1]:
        acc += w
        cuts.append(acc)
    return jnp.split(p, cuts, axis=-1)


def _heads(t, n_heads, d):
    return t.reshape(t.shape[:2] + (n_heads, d))


def _grid_rope(seq_len):
    rows = seq_len // GRID_W
    row = jnp.repeat(jnp.arange(rows, dtype=jnp.float32), GRID_W)
    col = jnp.tile(jnp.arange(GRID_W, dtype=jnp.float32), rows)
    half = HEAD_DIM // 2
    inv_freq = ROPE_THETA ** (-jnp.arange(0, half, 2, dtype=jnp.float32) / half)
    ang = jnp.concatenate([row[:, None] * inv_freq, col[:, None] * inv_freq], axis=-1)
    return jnp.cos(ang), jnp.sin(ang)


def _apply_rope(x, cos, sin):
    xf = x.astype(jnp.float32).reshape(x.shape[:-1] + (HEAD_DIM // 2, 2))
    x0, x1 = xf[..., 0], xf[..., 1]
    c, s = cos[:, None, :], sin[:, None, :]
    out = jnp.stack([x0 * c - x1 * s, x0 * s + x1 * c], axis=-1)
    return out.reshape(x.shape).astype(x.dtype)


def _attend_blocks(q, k, v):
    b, l = q.shape[:2]
    nb = l // Q_BLOCK
    qb = jnp.moveaxis(q.reshape(b, nb, Q_BLOCK, ATTN_KV_HEADS, ATTN_GROUPS, HEAD_DIM), 1, 0)
    scale = HEAD_DIM ** -0.5

    def one_block(qblk):
        s = jnp.einsum("bqkgd,bskd->bkgqs", qblk, k).astype(jnp.float32) * scale
        p = jax.nn.softmax(s, axis=-1)
        return jnp.einsum("bkgqs,bskd->bqkgd", p.astype(v.dtype), v)

    out = lax.map(one_block, qb)
    return jnp.moveaxis(out, 0, 1).reshape(b, l, ATTN_WIDTH)


def _retention_chunkwise(q, k, v, log_gamma, state0):
    b, h, l, _ = q.shape
    n = l // RET_CHUNK
    idx = jnp.arange(RET_CHUNK, dtype=jnp.float32)
    diff = idx[:, None] - idx[None, :]
    lower = diff >= 0
    intra = jnp.where(lower[None], jnp.exp(jnp.where(lower, diff, 0.0)[None] * log_gamma[:, None, None]), 0.0)
    q_dec = jnp.exp((idx + 1.0)[None, :] * log_gamma[:, None])
    k_dec = jnp.exp((RET_CHUNK - 1.0 - idx)[None, :] * log_gamma[:, None])
    chunk_dec = jnp.exp(RET_CHUNK * log_gamma)

    def chunks(t):
        return jnp.moveaxis(t.reshape(b, h, n, RET_CHUNK, t.shape[-1]), 2, 0)

    def step(state, qkv):
        qc, kc, vc = qkv
        inner = jnp.einsum("bhid,bhjd->bhij", qc, kc) * intra
        y = jnp.einsum("bhij,bhje->bhie", inner, vc) + jnp.einsum("bhid,bhde->bhie", qc, state) * q_dec[..., None]
        state = state * chunk_dec[:, None, None] + jnp.einsum("bhjd,bhje->bhde", kc * k_dec[..., None], vc)
        return state, y

    _, ys = lax.scan(step, state0, (chunks(q), chunks(k), chunks(v)))
    return jnp.moveaxis(ys, 0, 2).reshape(b, h, l, v.shape[-1])


def _retention_state(k, v, log_gamma, reverse):
    l = k.shape[2]
    m = jnp.arange(l, dtype=jnp.float32)
    expo = m if reverse else (l - 1.0 - m)
    w = jnp.exp(expo[None, :] * log_gamma[:, None])
    return jnp.einsum("bhld,bhle,hl->bhde", k, v, w)


def _retention_bidir(q, k, v, lg_fwd, lg_bwd, state_fwd, state_bwd):
    flip = lambda t: jnp.flip(t, axis=2)
    y_f = _retention_chunkwise(q, k, v, lg_fwd, state_fwd)
    y_b = flip(_retention_chunkwise(flip(q), flip(k), flip(v), lg_bwd, state_bwd))
    return y_f + y_b


def _ret_heads(t, d):
    return jnp.transpose(_heads(t, RET_HEADS, d), (0, 2, 1, 3)).astype(jnp.float32)


def _retention_out(y, gate):
    b, h, l, d = y.shape
    y = jnp.transpose(_rms(y), (0, 2, 1, 3)).reshape(b, l, h * d).astype(gate.dtype)
    return jax.nn.silu(gate) * y


def _merge(y_attn, y_ret, g_attn, g_ret, w_proj_attn, w_proj_ret, w_out):
    return (jax.nn.sigmoid(g_attn) * (y_attn @ w_proj_attn) + jax.nn.sigmoid(g_ret) * (y_ret @ w_proj_ret)) @ w_out


def _mixer(n_x, n_c, w_in, q_gain, k_gain, decay_logit, w_proj_attn, w_proj_ret, w_out, with_ctx_out):
    seq_len = n_x.shape[1]
    qa_x, ka_x, va_x, qr_x, kr_x, vr_x, gr_x, ga_x, gb_x = _split_proj(n_x @ w_in)
    qa_c, ka_c, va_c, qr_c, kr_c, vr_c, gr_c, ga_c, gb_c = _split_proj(n_c @ w_in)

    cos, sin = _grid_rope(seq_len)
    q_x = _apply_rope(_rms(_heads(qa_x, ATTN_Q_HEADS, HEAD_DIM)) * q_gain, cos, sin)
    k_x = _apply_rope(_rms(_heads(ka_x, ATTN_KV_HEADS, HEAD_DIM)) * k_gain, cos, sin)
    k_c = _rms(_heads(ka_c, ATTN_KV_HEADS, HEAD_DIM)) * k_gain
    v_x = _heads(va_x, ATTN_KV_HEADS, HEAD_DIM)
    v_c = _heads(va_c, ATTN_KV_HEADS, HEAD_DIM)
    k_all = jnp.concatenate([k_c, k_x], axis=1)
    v_all = jnp.concatenate([v_c, v_x], axis=1)
    ya_x = _attend_blocks(q_x, k_all, v_all)

    log_gamma = jax.nn.log_sigmoid(decay_logit.astype(jnp.float32))
    k_scale = RET_QK_DIM ** -0.5
    qr_xh, kr_xh, vr_xh = _ret_heads(qr_x, RET_QK_DIM), _ret_heads(kr_x, RET_QK_DIM) * k_scale, _ret_heads(vr_x, RET_V_DIM)
    kr_ch, vr_ch = _ret_heads(kr_c, RET_QK_DIM) * k_scale, _ret_heads(vr_c, RET_V_DIM)
    state_f = _retention_state(kr_ch, vr_ch, log_gamma[0], False)
    state_b = _retention_state(kr_ch, vr_ch, log_gamma[1], True)
    yr_x = _retention_out(_retention_bidir(qr_xh, kr_xh, vr_xh, log_gamma[0], log_gamma[1], state_f, state_b), gr_x)

    out_x = _merge(ya_x, yr_x, ga_x, gb_x, w_proj_attn, w_proj_ret, w_out)
    if not with_ctx_out:
        return out_x, None

    q_c = _rms(_heads(qa_c, ATTN_Q_HEADS, HEAD_DIM)) * q_gain
    ya_c = _attend_blocks(q_c, k_c, v_c)
    zeros = jnp.zeros_like(state_f)
    qr_ch = _ret_heads(qr_c, RET_QK_DIM)
    yr_c = _retention_out(_retention_bidir(qr_ch, kr_ch, vr_ch, log_gamma[0], log_gamma[1], zeros, zeros), gr_c)
    out_c = _merge(ya_c, yr_c, ga_c, gb_c, w_proj_attn, w_proj_ret, w_out)
    return out_x, out_c


def setup_inputs(seed: int = 0) -> dict:
    key = jax.random.key(seed)
    ks = jax.random.split(key, 18)

    def nrm(k, shape, std):
        return jax.random.normal(k, shape, jnp.float32) * std

    heads_idx = jnp.arange(RET_HEADS, dtype=jnp.float32)
    decay_base = jnp.log1p(-(2.0 ** (-(5.0 + heads_idx)))) + (5.0 + heads_idx) * jnp.log(2.0)
    return {
        "x": nrm(ks[0], (BATCH, SEQ, D_MODEL), 1.0),
        "c": nrm(ks[1], (BATCH, D_MODEL), 1.0),
        "ctx": nrm(ks[2], (BATCH, CTX_LEN, D_MODEL), 1.0),
        "c_ctx": nrm(ks[3], (D_MODEL,), 1.0),
        "w_ada": nrm(ks[4], (DEPTH, D_MODEL, N_MOD * D_MODEL), 0.5 * D_MODEL ** -0.5),
        "b_ada": nrm(ks[5], (DEPTH, N_MOD * D_MODEL), 0.01),
        "ffn1_w_in": nrm(ks[6], (DEPTH, D_MODEL, 2 * D_FF), D_MODEL ** -0.5),
        "ffn1_w_out": nrm(ks[7], (DEPTH, D_FF, D_MODEL), D_FF ** -0.5),
        "mix_w_in": nrm(ks[8], (DEPTH, D_MODEL, PROJ_WIDTH), D_MODEL ** -0.5),
        "attn_q_gain": 1.0 + nrm(ks[9], (DEPTH, HEAD_DIM), 0.02),
        "attn_k_gain": 1.0 + nrm(ks[10], (DEPTH, HEAD_DIM), 0.02),
        "ret_decay_logit": decay_base[None, None, :] + nrm(ks[11], (DEPTH, 2, RET_HEADS), 0.1),
        "w_proj_attn": nrm(ks[12], (DEPTH, ATTN_WIDTH, D_MODEL), ATTN_WIDTH ** -0.5),
        "w_proj_ret": nrm(ks[13], (DEPTH, RET_V_WIDTH, D_MODEL), RET_V_WIDTH ** -0.5),
        "mix_w_out": nrm(ks[14], (DEPTH, D_MODEL, D_MODEL), D_MODEL ** -0.5),
        "ffn2_w_in": nrm(ks[15], (DEPTH, D_MODEL, 2 * D_FF), D_MODEL ** -0.5),
        "ffn2_w_out": nrm(ks[16], (DEPTH, D_FF, D_MODEL), D_FF ** -0.5),
        "final_norm": 1.0 + nrm(ks[17], (D_MODEL,), 0.02),
    }


def reference(x, c, ctx, c_ctx, w_ada, b_ada, ffn1_w_in, ffn1_w_out, mix_w_in, attn_q_gain, attn_k_gain,
              ret_decay_logit, w_proj_attn, w_proj_ret, mix_w_out, ffn2_w_in, ffn2_w_out, final_norm):
    h_x = x
    h_c = ctx
    for layer in range(DEPTH):
        last = layer == DEPTH - 1
        sh1, sc1, g1, sh2, sc2, g2, sh3, sc3, g3 = [t[:, None, :] for t in _adaln(c, w_ada[layer], b_ada[layer])]
        csh1, csc1, cg1, csh2, csc2, cg2, csh3, csc3, cg3 = _adaln(c_ctx, w_ada[layer], b_ada[layer])

        h_x = h_x + 0.5 * g1 * _swiglu(_modulate(_rms(h_x), sh1, sc1), ffn1_w_in[layer], ffn1_w_out[layer])
        h_c = h_c + 0.5 * cg1 * _swiglu(_modulate(_rms(h_c), csh1, csc1), ffn1_w_in[layer], ffn1_w_out[layer])

        y_x, y_c = _mixer(_modulate(_rms(h_x), sh2, sc2), _modulate(_rms(h_c), csh2, csc2),
                          mix_w_in[layer], attn_q_gain[layer], attn_k_gain[layer], ret_decay_logit[layer],
                          w_proj_attn[layer], w_proj_ret[layer], mix_w_out[layer], not last)
        h_x = h_x + g2 * y_x

        h_x = h_x + 0.5 * g3 * _swiglu(_modulate(_rms(h_x), sh3, sc3), ffn2_w_in[layer], ffn2_w_out[layer])
        if not last:
            h_c = h_c + cg2 * y_c
            h_c = h_c + 0.5 * cg3 * _swiglu(_modulate(_rms(h_c), csh3, csc3), ffn2_w_in[layer], ffn2_w_out[layer])
    return _rms(h_x) * final_norm
```

```python
import contextlib
import numpy as np
import concourse.bass as bass
import concourse.mybir as mybir
from concourse.bass_utils import run_bass_kernel_spmd

F32 = mybir.dt.float32
BF16 = mybir.dt.bfloat16
AF = mybir.ActivationFunctionType
ALU = mybir.AluOpType
AX = mybir.AxisListType

D = 2048
DFF = 5632
NF = DFF // 128
SEQ = 8192
HALF = 4096
CTX = 256
NKEY = CTX + SEQ
PW = 9728
EPS = 1e-6
ENGS = ("pe", "act", "dve", "pool", "sp")
_FL = ["INTER"]
STQ = "act" if "ACTQ" in _FL else "pool"


class Rec:
    __slots__ = ("eng", "fn", "deps", "signal", "value", "is_dma", "sem", "idx")


class Buf:
    __slots__ = ("name", "writers", "readers", "prev")

    def __init__(self, name=""):
        self.name = name
        self.writers = []
        self.readers = []
        self.prev = []


class DmaSlot:
    __slots__ = ("sem", "count")

    def __init__(self, sem):
        self.sem = sem
        self.count = 0


def _compress(lst):
    best = {}
    for r in lst:
        key = ("d", id(r.sem)) if r.is_dma else ("e", r.eng)
        o = best.get(key)
        if o is None:
            best[key] = r
        elif r.is_dma:
            if r.value > o.value:
                best[key] = r
        elif r.idx > o.idx:
            best[key] = r
    return list(best.values())


class Prog:
    def __init__(self, nc):
        self.nc = nc
        self.q = {e: [] for e in ENGS}

    def _add(self, eng, fn, reads, writes, extra):
        r = Rec()
        r.eng = eng
        r.fn = fn
        r.signal = False
        r.value = 0
        r.is_dma = False
        r.sem = None
        r.idx = len(self.q[eng])
        deps = list(extra) if extra else []
        for b in reads:
            deps.extend(b.writers)
        for b in writes:
            if b.readers:
                b.prev = _compress(b.readers + b.writers)
                b.readers = []
                b.writers = []
            deps.extend(b.prev)
        r.deps = deps
        self.q[eng].append(r)
        return r

    def _post(self, r, reads, writes):
        for b in reads:
            b.readers.append(r)
            if len(b.readers) > 8:
                b.readers = _compress(b.readers)
        for b in writes:
            b.writers.append(r)
            if len(b.writers) > 8:
                b.writers = _compress(b.writers)

    def op(self, eng, name, *args, reads=(), writes=(), extra=None, **kw):
        def fn(e, name=name, args=args, kw=kw):
            return getattr(e, name)(*args, **kw)
        r = self._add(eng, fn, reads, writes, extra)
        self._post(r, reads, writes)
        return r

    def dma(self, eng, slot, out, in_, reads=(), writes=(), extra=None):
        def fn(e, out=out, in_=in_):
            return e.dma_start(out=out, in_=in_)
        r = self._add(eng, fn, reads, writes, extra)
        r.is_dma = True
        r.sem = slot.sem
        slot.count += 16
        r.value = slot.count
        self._post(r, reads, writes)
        return r

    def barrier(self):
        marks = []
        for e in ENGS:
            for r in reversed(self.q[e]):
                if not r.is_dma and r.fn is not None:
                    marks.append(r)
                    break
        for e in ENGS:
            best = {}
            for r in self.q[e]:
                if r.is_dma:
                    o = best.get(id(r.sem))
                    if o is None or r.value > o.value:
                        best[id(r.sem)] = r
            marks.extend(best.values())
        for e in ENGS:
            r = Rec()
            r.eng = e
            r.fn = None
            r.signal = False
            r.value = 0
            r.is_dma = False
            r.sem = None
            r.idx = len(self.q[e])
            r.deps = list(marks)
            self.q[e].append(r)

    def finalize(self, block, engsem):
        for e in ENGS:
            for r in self.q[e]:
                for d in r.deps:
                    if not d.is_dma:
                        d.signal = True
        for e in ENGS:
            cnt = 0
            for r in self.q[e]:
                if r.is_dma or r.fn is None:
                    continue
                if r.signal:
                    cnt += 1
                    r.value = cnt
                    r.sem = engsem[e]
        for e in ENGS:
            recs = self.q[e]

            def body(eng, recs=recs):
                waited = {}
                for r in recs:
                    need = {}
                    for d in r.deps:
                        k = id(d.sem)
                        o = need.get(k)
                        if o is None or o[1] < d.value:
                            need[k] = (d.sem, d.value)
                    for k, (sem_, val_) in need.items():
                        if waited.get(k, 0) < val_:
                            eng.wait_ge(sem_, val_)
                            waited[k] = val_
                    if r.fn is None:
                        continue
                    ins = r.fn(eng)
                    if r.is_dma:
                        ins.then_inc(r.sem, 16)
                    elif r.signal:
                        ins.then_inc(r.sem, 1)

            if not recs:
                continue
            {"pe": block.tensor, "act": block.scalar, "dve": block.vector,
             "pool": block.gpsimd, "sp": block.sync}[e](body)


class Arena:
    def __init__(self, ap, nwords):
        self.ap = ap
        self.n = nwords
        self.off = 0

    def reset(self, off=0):
        self.off = off

    def f32(self, *shape):
        n = int(np.prod(shape))
        assert self.off + n <= self.n, ("arena overflow", self.off, n, self.n)
        v = self.ap[:, self.off:self.off + n]
        self.off += n
        return self._shape(v, shape)

    def bf16(self, *shape):
        n = int(np.prod(shape))
        w = (n + 1) // 2
        assert self.off + w <= self.n, ("arena overflow", self.off, w, self.n)
        v = self.ap[:, self.off:self.off + w].bitcast(BF16)[:, 0:n]
        self.off += w
        return self._shape(v, shape)

    @staticmethod
    def _shape(v, shape):
        if len(shape) == 1:
            return v
        if len(shape) == 2:
            return v.rearrange("p (a b) -> p a b", a=shape[0])
        if len(shape) == 3:
            return v.rearrange("p (a b c) -> p a b c", a=shape[0], b=shape[1])
        raise ValueError(shape)


def bc(ap, shape):
    return ap.to_broadcast(list(shape))


def build_program():
    nc = bass.Bass("TRN2", target_bir_lowering=False)

    def din(name, shape):
        return nc.dram_tensor(name, list(shape), F32, kind="ExternalInput").ap()

    x_all = din("x_all", [NKEY, D])
    rope = din("rope", [NKEY, 128])
    cT = din("cT", [128, 32])
    b_adaT = din("b_adaT", [128, 144])
    b_ada = din("b_ada", [1, 9 * D])
    w_ada = din("w_ada", [D, 9 * D])
    w1i = din("ffn1_w_in", [D, 2 * DFF])
    w1o = din("ffn1_w_out", [DFF, D])
    wmix = din("mix_w_in", [D, PW])
    gains = din("gains", [128, 256])
    declg = din("declg", [128, 16])
    wpa = din("w_proj_attn", [1024, D])
    wpr = din("w_proj_ret", [1024, D])
    wmo = din("mix_w_out", [D, D])
    w2i = din("ffn2_w_in", [D, 2 * DFF])
    w2o = din("ffn2_w_out", [DFF, D])
    fnorm = din("fnorm", [128, D])
    rconst = din("rconst", [128, 1024])
    hfin = din("hf", [128, 1])
    out = nc.dram_tensor("out", [HALF, D], F32, kind="ExternalOutput").ap()

    def dscr(name, shape, dt=BF16):
        return nc.dram_tensor(name, list(shape), dt).ap()

    w1i_b = dscr("w1i_b", [D, 2 * DFF])
    w1o_b = dscr("w1o_b", [DFF, D])
    wmix_b = dscr("wmix_b", [D, PW])
    wpa_b = dscr("wpa_b", [1024, D])
    wpr_b = dscr("wpr_b", [1024, D])
    wmo_b = dscr("wmo_b", [D, D])
    w2i_b = dscr("w2i_b", [D, 2 * DFF])
    w2o_b = dscr("w2o_b", [DFF, D])
    h1_s = dscr("h1_s", [HALF, D], F32)
    qT_s = dscr("qT_s", [8, 128, HALF])
    kT_s = dscr("kT_s", [2, 128, NKEY])
    v_s = dscr("v_s", [NKEY, 256])
    qrT_s = dscr("qrT_s", [8, 128, HALF])
    krT_s = dscr("krT_s", [8, 128, HALF])
    kr_s = dscr("kr_s", [NKEY, 1024])
    vr_s = dscr("vr_s", [NKEY, 1024])
    sgr_s = dscr("sgr_s", [HALF, 1024])
    sgaT_s = dscr("sgaT_s", [16, 128, HALF])
    sgbT_s = dscr("sgbT_s", [16, 128, HALF])
    yaT_s = dscr("yaT_s", [8, 128, HALF])
    yrT_s = dscr("yrT_s", [8, 128, HALF])

    NW = 52000
    with contextlib.ExitStack() as st:
        arena_t = st.enter_context(nc.sbuf_tensor("arena", [128, NW], F32))
        AR = Arena(arena_t, NW)
        psb = [st.enter_context(nc.psum_tensor(f"psb{i}", [128, 512], F32)) for i in range(8)]
        engsem = {e: st.enter_context(nc.semaphore(f"es_{e}")) for e in ENGS}
        nslots = [0]

        def new_slot():
            nslots[0] += 1
            return DmaSlot(st.enter_context(nc.semaphore(f"ds{nslots[0]}")))

        block = st.enter_context(nc.Block())
        P = Prog(nc)
        pbank = [Buf(f"bank{i}") for i in range(8)]

        def PS(i):
            return psb[i][:, :]

        def PSB(i):
            return psb[i][:, :].bitcast(BF16)

        ident = AR.bf16(128)
        ones_b = AR.bf16(128)
        identf = AR.f32(128)
        b_ident = Buf()
        P.op("pool", "memset", identf, 0.0, writes=[b_ident])
        P.op("pool", "affine_select", out=identf, in_=identf, pattern=[[-1, 128]],
                                               compare_op=ALU.not_equal, fill=1.0, base=0,
                                               channel_multiplier=1, reads=[b_ident], writes=[b_ident])
        P.op("dve", "tensor_copy", out=ident, in_=identf, reads=[b_ident], writes=[b_ident])
        P.op("dve", "memset", ones_b, 1.0, writes=[b_ident])
        modT = AR.f32(6, 16, 2)
        b_modT = [Buf() for _ in range(6)]
        badT = AR.f32(144)
        b_badT = Buf()
        s_misc = new_slot()
        P.dma("sp", new_slot(), badT, b_adaT[:, :], writes=[b_badT])
        sc32 = AR.f32(32)
        scb = AR.bf16(16, 2)
        screp = AR.bf16(2, 16, 128)
        b_sc = Buf()
        P.dma("sp", new_slot(), sc32, cT[:, :], writes=[b_sc])
        sig32 = AR.f32(32)
        P.op("act", "activation", out=sig32, in_=sc32, func=AF.Silu, reads=[b_sc], writes=[b_sc])
        P.op("dve", "tensor_copy", out=scb, in_=sig32.rearrange("p (k w) -> p k w", w=2),
             reads=[b_sc], writes=[b_sc])
        for w in range(2):
            P.op("dve", "tensor_copy",
                out=screp[:, w], in_=bc(sig32.rearrange("p (k w) -> p k w", w=2)[:, :, w:w + 1], [128, 16, 128]),
                reads=[b_sc], writes=[b_sc])
        hf_t = AR.f32(1)
        b_hf = Buf()
        P.dma("sp", new_slot(), hf_t, hfin[:, :], writes=[b_hf])
        persist_off = AR.off

        def prepass(src, dst, rows, bufobj):
            sl = new_slot()
            for r0 in range(0, rows, 128):
                P.dma("pool", sl, dst[r0:r0 + 128, :], src[r0:r0 + 128, :], writes=[bufobj])

        b_w1i, b_w1o, b_wmix, b_wpa, b_wpr, b_wmo, b_w2i, b_w2o = [Buf() for _ in range(8)]
        NRING = 2
        GT_WORDS = NF * 512 // 2

        class Ctx:
            pass

        C = Ctx()

        def alloc_common():
            C.wring = [AR.bf16(16, 512) for _ in range(NRING)]
            C.b_wring = [Buf() for _ in range(NRING)]
            C.s_wring = [new_slot() for _ in range(NRING)]
            C.wring_i = 0
            C.woring = [AR.bf16(4, 512) for _ in range(3)]
            C.b_woring = [Buf() for _ in range(3)]
            C.s_woring = [new_slot() for _ in range(3)]
            C.woring_i = 0
            C.X = AR.f32(4, D)
            C.b_X = [Buf() for _ in range(4)]
            C.s_X = [new_slot() for _ in range(4)]
            C.xn = [AR.bf16(D) for _ in range(2)]
            C.b_xn = [Buf() for _ in range(2)]
            C.xn_i = 0
            C.ss = AR.f32(8)
            C.b_ss = Buf()
            C.b_ss2 = [Buf(), Buf()]
            C.gT = AR.bf16(NF, 512)
            C.b_gT = Buf()
            C.silu = [AR.f32(512) for _ in range(2)]
            C.b_silu = [Buf() for _ in range(2)]
            C.silu_i = 0
            C.G = [AR.f32(D) for _ in range(2)]
            C.b_G = [Buf() for _ in range(2)]
            C.tmpe = [AR.f32(512) for _ in range(2)]
            C.b_tmpe = [Buf() for _ in range(2)]
            C.tmpe_i = 0
            C.brow = AR.bf16(512)
            C.b_brow = Buf()
            C.s_brow = new_slot()

        def next_wslot():
            i = C.wring_i % NRING
            C.wring_i += 1
            return i

        def ada_mod(m, mi, plus1):
            bank = 3
            for qd in range(4):
                i = next_wslot()
                P.dma("pool", C.s_wring[i], C.wring[i],
                      w_ada[:, m * D + qd * 512: m * D + (qd + 1) * 512].rearrange("(kc p) n -> p kc n", p=128),
                      writes=[C.b_wring[i]])
                for c4 in range(4):
                    nci = qd * 4 + c4
                    for kc in range(16):
                        P.op("pe", "matmul",
                            PS(bank)[:, nci * 2:nci * 2 + 2], C.wring[i][:, kc, c4 * 128:(c4 + 1) * 128],
                            scb[:, kc, :], start=(kc == 0), stop=(kc == 15),
                            reads=[C.b_wring[i], b_sc], writes=[pbank[bank]])
            P.op("dve", "tensor_tensor",
                out=modT[:, mi], in0=PS(bank)[:, 0:32].rearrange("p (n w) -> p n w", w=2),
                in1=bc(badT[:, m * 16:(m + 1) * 16].unsqueeze(2), [128, 16, 2]), op=ALU.add,
                reads=[pbank[bank], b_badT], writes=[b_modT[mi]])
            if plus1:
                P.op("dve", "tensor_scalar_add", modT[:, mi], modT[:, mi], 1.0,
                     reads=[b_modT[mi]], writes=[b_modT[mi]])

        def ada_gate(m, targets):
            for qd in range(4):
                i = next_wslot()
                c0 = m * D + qd * 512
                P.dma("pool", C.s_wring[i], C.wring[i],
                      w_ada[:, c0:c0 + 512].rearrange("(kc p) n -> p kc n", p=128), writes=[C.b_wring[i]])
                P.dma("pool", C.s_brow, C.brow[0:1, :], b_ada[0:1, c0:c0 + 512], writes=[C.b_brow])
                for ti, (which, gi, scale) in enumerate(targets):
                    bank = 2 + ti
                    for kc in range(16):
                        P.op("pe", "matmul",
                            PS(bank), screp[:, which, kc, :], C.wring[i][:, kc, :], start=(kc == 0), stop=False,
                            reads=[C.b_wring[i], b_sc], writes=[pbank[bank]])
                    P.op("pe", "matmul", PS(bank), ones_b[0:1, :], C.brow[0:1, :],
                                                               start=False, stop=True,
                         reads=[C.b_brow, b_ident], writes=[pbank[bank]])
                    P.op("act", "activation",
                        out=C.G[gi][:, qd * 512:(qd + 1) * 512], in_=PS(bank), func=AF.Copy, scale=scale,
                        reads=[pbank[bank]], writes=[C.b_G[gi]])

        def load_x(row0, nt):
            for t in range(nt):
                P.dma("sp", C.s_X[t], C.X[:, t], x_all[row0 + t * 128: row0 + (t + 1) * 128, :],
                      writes=[C.b_X[t]])

        def norm_T(nt, mi, which, nT, b_nT, src_bufs=None):
            xis = {}

            def stage1(t):
                xi = C.xn_i % 2
                C.xn_i += 1
                xis[t] = xi
                bx = C.b_X[t]
                sv = C.ss[:, 4 * xi:4 * xi + 4]
                bs = C.b_ss2[xi]
                P.op("act", "activation", out=C.xn[xi], in_=C.X[:, t], func=AF.Square, accum_out=sv[:, 0:1],
                     reads=[bx], writes=[bs, C.b_xn[xi]])
                P.op("act", "activation", out=sv[:, 1:2], in_=sv[:, 0:1], func=AF.Sqrt, scale=1.0 / D, bias=EPS,
                     reads=[bs], writes=[bs])
                P.op("dve", "reciprocal", out=sv[:, 2:3], in_=sv[:, 1:2], reads=[bs], writes=[bs])
                P.op("dve", "tensor_scalar", out=C.xn[xi], in0=C.X[:, t], scalar1=sv[:, 2:3], scalar2=None,
                     op0=ALU.mult, reads=[bx, bs], writes=[C.b_xn[xi]])

            def stage2(t):
                xi = xis[t]
                bk = 4 + 2 * (t % 2)
                for kc in range(16):
                    b_ = bk + kc // 8
                    P.op("pe", "transpose",
                         PSB(b_)[:, (kc % 8) * 128:(kc % 8 + 1) * 128], C.xn[xi][:, kc * 128:(kc + 1) * 128], ident,
                         reads=[C.b_xn[xi], b_ident], writes=[pbank[b_]])
                for kc in range(16):
                    b_ = bk + kc // 8
                    P.op("act", "activation",
                         out=nT[:, kc, t * 128:(t + 1) * 128], in_=PSB(b_)[:, (kc % 8) * 128:(kc % 8 + 1) * 128],
                         func=AF.Identity, scale=modT[:, mi * 2 + 1, kc, which:which + 1],
                         bias=modT[:, mi * 2, kc, which:which + 1],
                         reads=[pbank[b_], b_modT[mi * 2], b_modT[mi * 2 + 1]], writes=[b_nT])

            stage1(0)
            if nt > 1:
                stage1(1)
            for t in range(nt):
                stage2(t)
                if t + 2 < nt:
                    stage1(t + 2)

        def ffn(nt, nT, b_nT, wi_b, b_wi, wo_b, b_wo, gi):
            ntok = nt * 128
            for jp in range(NF // 2):
                i = next_wslot()
                P.dma("sp", C.s_wring[i], C.wring[i][:, :, 0:256],
                      wi_b[:, jp * 256:(jp + 1) * 256].rearrange("(kc p) n -> p kc n", p=128),
                      reads=[b_wi], writes=[C.b_wring[i]])
                P.dma("sp", C.s_wring[i], C.wring[i][:, :, 256:512],
                      wi_b[:, DFF + jp * 256: DFF + (jp + 1) * 256].rearrange("(kc p) n -> p kc n", p=128),
                      reads=[b_wi], writes=[C.b_wring[i]])
                for jj in range(2):
                    f = jp * 2 + jj
                    ba, bb = (0, 1) if f % 2 == 0 else (2, 3)
                    for kc in range(16):
                        P.op("pe", "matmul",
                            PS(ba)[:, :ntok], C.wring[i][:, kc, jj * 128:(jj + 1) * 128], nT[:, kc, :ntok],
                            start=(kc == 0), stop=(kc == 15),
                            reads=[C.b_wring[i], b_nT], writes=[pbank[ba]])
                    for kc in range(16):
                        P.op("pe", "matmul",
                            PS(bb)[:, :ntok], C.wring[i][:, kc, 256 + jj * 128:256 + (jj + 1) * 128],
                            nT[:, kc, :ntok], start=(kc == 0), stop=(kc == 15),
                            reads=[C.b_wring[i], b_nT], writes=[pbank[bb]])
                    si = C.silu_i % 2
                    C.silu_i += 1
                    P.op("act", "activation", out=C.silu[si][:, :ntok], in_=PS(ba)[:, :ntok],
                                                                       func=AF.Silu,
                         reads=[pbank[ba]], writes=[C.b_silu[si]])
                    P.op("dve", "tensor_tensor",
                        out=C.gT[:, f, :ntok], in0=C.silu[si][:, :ntok], in1=PS(bb)[:, :ntok], op=ALU.mult,
                        reads=[pbank[bb], C.b_silu[si]], writes=[C.b_gT])
            out_gemm(nt, lambda f, t: C.gT[:, f, t * 128:(t + 1) * 128], [C.b_gT], NF, wo_b, b_wo, gi)

        def out_gemm(nt, lhs_fn, lhs_bufs, nk, wo_b, b_wo, gi):
            for db in range(4):
                for fq in range(nk // 4):
                    i = C.woring_i % 3
                    C.woring_i += 1
                    P.dma("sp", C.s_woring[i], C.woring[i],
                          wo_b[fq * 512:(fq + 1) * 512, db * 512:(db + 1) * 512].rearrange("(f p) n -> p f n", p=128),
                          reads=[b_wo], writes=[C.b_woring[i]])
                    for fi in range(4):
                        f = fq * 4 + fi
                        for t in range(nt):
                            P.op("pe", "matmul",
                                PS(4 + t), lhs_fn(f, t), C.woring[i][:, fi, :], start=(f == 0), stop=(f == nk - 1),
                                reads=[C.b_woring[i]] + lhs_bufs, writes=[pbank[4 + t]])
                for t in range(nt):
                    ti = C.tmpe_i % 2
                    C.tmpe_i += 1
                    P.op("dve", "tensor_tensor",
                        out=C.tmpe[ti], in0=PS(4 + t), in1=C.G[gi][:, db * 512:(db + 1) * 512], op=ALU.mult,
                        reads=[pbank[4 + t], C.b_G[gi]], writes=[C.b_tmpe[ti]])
                    P.op("pool", "tensor_tensor",
                        out=C.X[:, t, db * 512:(db + 1) * 512], in0=C.X[:, t, db * 512:(db + 1) * 512],
                        in1=C.tmpe[ti], op=ALU.add,
                        reads=[C.b_tmpe[ti], C.b_X[t]], writes=[C.b_X[t]])

        alloc_common()
        nT2 = [AR.bf16(16, 512)] * 2
        b_nT2 = [Buf()] * 2
        A_sq = AR.f32(512)
        A_qn = AR.f32(512)
        A_t = [AR.f32(256) for _ in range(4)]
        A_ro = AR.bf16(512)
        A_cs = [AR.f32(128) for _ in range(2)]
        b_cs = [Buf() for _ in range(2)]
        s_cs = [new_slot() for _ in range(2)]
        A_r = AR.f32(16)
        b_qk = Buf()
        gain_t = AR.f32(256)
        b_gain = Buf()
        P.dma("sp", new_slot(), gain_t, gains[:, :], writes=[b_gain])
        NST = 4
        stg = [AR.bf16(512) for _ in range(NST)]
        b_stg = [Buf() for _ in range(NST)]
        s_stg = [new_slot() for _ in range(NST)]
        stg_i = [0]
        s_h1 = [new_slot() for _ in range(4)]
        b_scr = Buf("scratch_all")

        def next_stg():
            i = stg_i[0] % NST
            stg_i[0] += 1
            return i

        gflatA = C.gT.rearrange("p f t -> p (f t)")
        pslots = [(C.wring[k_], C.b_wring[k_], C.s_wring[k_]) for k_ in range(NRING)]
        alias_bufs = []
        for a_ in range(2):
            ab_ = Buf()
            alias_bufs.append(ab_)
            pslots.append((gflatA[:, a_ * 8192:(a_ + 1) * 8192].rearrange("p (k n) -> p k n", k=16), ab_, new_slot()))
        pslot_i = [0]
        rslot_i = [0]

        def handoff(srcs, dsts):
            acc = []
            for b_ in srcs:
                acc += b_.readers + b_.writers + b_.prev
            for d_ in dsts:
                d_.prev = _compress(acc + d_.readers + d_.writers + d_.prev)
                d_.readers = []
                d_.writers = []

        class LazyUnit:
            def __init__(self, c0, rope=False):
                self.c0 = c0
                self.u = None
                self.rope = rope

            def get(self):
                if self.u is None:
                    if self.rope:
                        ap, bf, sl = pslots[2 + rslot_i[0] % 2]
                        rslot_i[0] += 1
                    else:
                        ap, bf, sl = pslots[pslot_i[0] % 2]
                        pslot_i[0] += 1
                    P.dma("sp", sl, ap, wmix_b[:, self.c0:self.c0 + 512].rearrange("(kc p) n -> p kc n", p=128),
                          reads=[b_wmix], writes=[bf])
                    self.u = (ap, bf)
                return self.u

        def tm_gemm(u, t, bank, nT, b_nT):
            ap, bf = u
            for kc in range(16):
                P.op("pe", "matmul", PS(bank), nT[:, kc, t * 128:(t + 1) * 128], ap[:, kc, :],
                     start=(kc == 0), stop=(kc == 15), reads=[bf, b_nT], writes=[pbank[bank]])

        def fm_gemm(u, c4, ntok, bank, nT, b_nT):
            ap, bf = u
            for kc in range(16):
                P.op("pe", "matmul", PS(bank)[:, :ntok], ap[:, kc, c4 * 128:(c4 + 1) * 128],
                     nT[:, kc, :ntok], start=(kc == 0), stop=(kc == 15), reads=[bf, b_nT], writes=[pbank[bank]])

        def evac_store(bank, ncols, func, scale, dst):
            si = next_stg()
            P.op("act", "activation", out=stg[si][:, :ncols], in_=PS(bank)[:, :ncols], func=func, scale=scale,
                 reads=[pbank[bank]], writes=[b_stg[si]])
            P.dma(STQ, s_stg[si], dst, stg[si][:, :ncols], reads=[b_stg[si]], writes=[b_scr])

        def qk_norm_rope(bank, nh, goff, csi):
            w = nh * 128
            pv = PS(bank)[:, :w].rearrange("p (h d) -> p h d", h=nh)
            P.op("act", "activation", out=A_sq[:, :w], in_=PS(bank)[:, :w], func=AF.Square,
                 reads=[pbank[bank]], writes=[b_qk])
            P.op("dve", "tensor_reduce", out=A_r[:, 0:nh], in_=A_sq[:, :w].rearrange("p (h d) -> p h d", h=nh),
                                                  axis=AX.X, op=ALU.add, reads=[b_qk], writes=[b_qk])
            P.op("act", "activation", out=A_r[:, 4:4 + nh], in_=A_r[:, 0:nh], func=AF.Sqrt,
                                               scale=1.0 / 128, bias=EPS, reads=[b_qk], writes=[b_qk])
            P.op("dve", "reciprocal", out=A_r[:, 8:8 + nh], in_=A_r[:, 4:4 + nh], reads=[b_qk], writes=[b_qk])
            qn = A_qn[:, :w].rearrange("p (h d) -> p h d", h=nh)
            P.op("dve", "tensor_tensor", out=qn, in0=pv, in1=bc(A_r[:, 8:8 + nh].unsqueeze(2), [128, nh, 128]),
                                                  op=ALU.mult, reads=[pbank[bank], b_qk], writes=[b_qk])
            P.op("pool", "tensor_tensor", out=qn, in0=qn,
                                                   in1=bc(gain_t[:, goff:goff + 128].unsqueeze(1), [128, nh, 128]),
                                                   op=ALU.mult, reads=[b_qk, b_gain], writes=[b_qk])
            q4 = A_qn[:, :w].rearrange("p (h i two) -> p h i two", h=nh, two=2)
            o4 = A_ro[:, :w].rearrange("p (h i two) -> p h i two", h=nh, two=2)
            cosb = bc(A_cs[csi][:, 0:64].unsqueeze(1), [128, nh, 64])
            sinb = bc(A_cs[csi][:, 64:128].unsqueeze(1), [128, nh, 64])
            tv = [A_t[k][:, :nh * 64].rearrange("p (h i) -> p h i", h=nh) for k in range(4)]
            rd = [b_qk, b_cs[csi]]
            P.op("dve", "tensor_tensor", out=tv[0], in0=q4[:, :, :, 0], in1=cosb, op=ALU.mult, reads=rd, writes=[b_qk])
            P.op("pool", "tensor_tensor", out=tv[1], in0=q4[:, :, :, 1], in1=sinb, op=ALU.mult, reads=rd, writes=[b_qk])
            P.op("pool", "tensor_tensor", out=tv[2], in0=q4[:, :, :, 0], in1=sinb, op=ALU.mult, reads=rd, writes=[b_qk])
            P.op("dve", "tensor_tensor", out=tv[3], in0=q4[:, :, :, 1], in1=cosb, op=ALU.mult, reads=rd, writes=[b_qk])
            P.op("dve", "tensor_tensor", out=o4[:, :, :, 0], in0=tv[0], in1=tv[1], op=ALU.subtract,
                 reads=[b_qk], writes=[b_qk])
            P.op("pool", "tensor_tensor", out=o4[:, :, :, 1], in0=tv[2], in1=tv[3], op=ALU.add,
                 reads=[b_qk], writes=[b_qk])

        def transpose_store(nh, bank, dst):
            for h in range(nh):
                P.op("pe", "transpose", PSB(bank)[:, h * 128:(h + 1) * 128], A_ro[:, h * 128:(h + 1) * 128], ident,
                     reads=[b_qk, b_ident], writes=[pbank[bank]])
            si = next_stg()
            P.op("act", "activation", out=stg[si][:, :nh * 128], in_=PSB(bank)[:, :nh * 128], func=AF.Copy,
                 reads=[pbank[bank]], writes=[b_stg[si]])
            P.dma(STQ, s_stg[si], dst, stg[si][:, :nh * 128].rearrange("p (h t) -> p h t", h=nh),
                  reads=[b_stg[si]], writes=[b_scr])

        KS = 128.0 ** -0.5

        def proj_block(nt, key0, own0, nT, b_nT):
            ntok = nt * 128
            own = own0 is not None
            pb_i = [0]
            rb_i = [0]

            def nbp():
                b_ = pb_i[0] % 4
                pb_i[0] += 1
                return b_

            rope_steps = []
            plain = []

            def add_rope(unit, t, nh, goff, dst, with_v):
                st_ = {}

                def G():
                    k_ = rb_i[0]
                    rb_i[0] += 1
                    csi = k_ % 2
                    bank = 4 + k_ % 2
                    st_["tb"] = 6 + k_ % 2
                    P.dma("sp", s_cs[csi], A_cs[csi], rope[key0 + t * 128: key0 + (t + 1) * 128, :], writes=[b_cs[csi]])
                    tm_gemm(unit.get(), t, bank, nT, b_nT)
                    if with_v:
                        si = next_stg()
                        P.op("act", "activation", out=stg[si][:, :256], in_=PS(bank)[:, 256:512], func=AF.Copy,
                             reads=[pbank[bank]], writes=[b_stg[si]])
                        P.dma(STQ, s_stg[si], v_s[key0 + t * 128: key0 + (t + 1) * 128, :], stg[si][:, :256],
                              reads=[b_stg[si]], writes=[b_scr])
                    qk_norm_rope(bank, nh, goff, csi)

                def F():
                    transpose_store(nh, st_["tb"], dst)
                rope_steps.append((G, F))

            def add_tm(unit, t, func, scale, dst):
                def it():
                    bank = nbp()
                    tm_gemm(unit.get(), t, bank, nT, b_nT)
                    evac_store(bank, 512, func, scale, dst)
                plain.append(it)

            def add_fm(unit, c4, func, scale, dst):
                def it():
                    bank = nbp()
                    fm_gemm(unit.get(), c4, ntok, bank, nT, b_nT)
                    evac_store(bank, ntok, func, scale, dst)
                plain.append(it)

            ukv = LazyUnit(1024, rope=True)
            for t in range(nt):
                add_rope(ukv, t, 2, 128, kT_s[:, :, key0 + t * 128: key0 + (t + 1) * 128].rearrange("h d t -> d h t"), True)
            if own:
                for u in range(2):
                    uq = LazyUnit(u * 512, rope=True)
                    for t in range(nt):
                        add_rope(uq, t, 4, 0, qT_s[u * 4:(u + 1) * 4, :, own0 + t * 128: own0 + (t + 1) * 128]
                                 .rearrange("h d t -> d h t"), False)
                for u in range(2):
                    un = LazyUnit(1536 + u * 512)
                    for c4 in range(4):
                        add_fm(un, c4, AF.Copy, 1.0, qrT_s[u * 4 + c4, :, own0:own0 + ntok])
            for u in range(2):
                un = LazyUnit(2560 + u * 512)
                for t in range(nt):
                    add_tm(un, t, AF.Copy, KS, kr_s[key0 + t * 128: key0 + (t + 1) * 128, u * 512:(u + 1) * 512])
                if own:
                    for c4 in range(4):
                        add_fm(un, c4, AF.Copy, KS, krT_s[u * 4 + c4, :, own0:own0 + ntok])
            for u in range(2):
                un = LazyUnit(3584 + u * 512)
                for t in range(nt):
                    add_tm(un, t, AF.Copy, 1.0, vr_s[key0 + t * 128: key0 + (t + 1) * 128, u * 512:(u + 1) * 512])
            if own:
                for u in range(2):
                    un = LazyUnit(4608 + u * 512)
                    for t in range(nt):
                        add_tm(un, t, AF.Silu, 1.0, sgr_s[own0 + t * 128: own0 + (t + 1) * 128, u * 512:(u + 1) * 512])
                for gsel, dstT in ((0, sgaT_s), (1, sgbT_s)):
                    for u in range(4):
                        un = LazyUnit(5632 + gsel * 2048 + u * 512)
                        for c4 in range(4):
                            add_fm(un, c4, AF.Sigmoid, 1.0, dstT[u * 4 + c4, :, own0:own0 + ntok])
            k_ = -(-len(plain) // len(rope_steps)) if "INTER" in _FL else 0
            pi = 0
            for (G, F) in rope_steps:
                G()
                for it in plain[pi:pi + k_]:
                    it()
                pi += k_
                F()
            for it in plain[pi:]:
                it()

        prepass(w1i, w1i_b, D, b_w1i)
        ada_mod(0, 0, False)
        ada_mod(1, 1, True)
        ada_gate(2, [(1, 0, 0.5), (0, 1, 0.5)])
        prepass(w1o, w1o_b, DFF, b_w1o)
        prepass(wmix, wmix_b, D, b_wmix)
        ada_mod(3, 2, False)
        ada_mod(4, 3, True)

        blocks = [(0, 2, 1, None)]
        for bI in range(8):
            blocks.append((CTX + bI * 512, 4, 0, None))
        for bI in range(8):
            blocks.append((CTX + HALF + bI * 512, 4, 0, bI * 512))
        for bidx, (row0, nt, which, own0) in enumerate(blocks):
            load_x(row0, nt)
            nTa, b_nTa = nT2[0], b_nT2[0]
            norm_T(nt, 0, which, nTa, b_nTa)
            handoff(alias_bufs, [C.b_gT])
            ffn(nt, nTa, b_nTa, w1i_b, b_w1i, w1o_b, b_w1o, 0 if which == 1 else 1)
            handoff([C.b_gT], alias_bufs)
            if own0 is not None:
                for t in range(nt):
                    P.dma("pool", s_h1[t], h1_s[own0 + t * 128: own0 + (t + 1) * 128, :], C.X[:, t],
                          reads=[C.b_X[t]], writes=[b_scr])
            nTb, b_nTb = nT2[1], b_nT2[1]
            norm_T(nt, 1, which, nTb, b_nTb)
            proj_block(nt, row0, own0, nTb, b_nTb)
            if bidx == 2:
                prepass(wpa, wpa_b, 1024, b_wpa)
                prepass(wpr, wpr_b, 1024, b_wpr)
                prepass(wmo, wmo_b, D, b_wmo)
                prepass(w2i, w2i_b, D, b_w2i)
                prepass(w2o, w2o_b, DFF, b_w2o)

        P.barrier()
        AR.reset(persist_off)
        kT_sb = AR.bf16(2, NKEY)
        v_sb = AR.bf16(NKEY // 128, 256)
        b_kv = Buf()
        s_kv = new_slot()
        for h in range(2):
            for c in range(0, NKEY, 2112):
                P.dma("sp", s_kv, kT_sb[:, h, c:c + 2112], kT_s[h, :, c:c + 2112], writes=[b_kv])
        for c in range(0, NKEY // 128, 11):
            P.dma("sp", s_kv, v_sb[:, c:c + 11, :], v_s[c * 128:(c + 11) * 128, :].rearrange("(t p) n -> p t n", p=128),
                  writes=[b_kv])
        qblk = [AR.bf16(512) for _ in range(2)]
        b_qblk = [Buf() for _ in range(2)]
        s_qblk = [new_slot() for _ in range(2)]
        NPT = 6
        PT = [AR.bf16(512) for _ in range(NPT)]
        b_PT = [Buf() for _ in range(NPT)]
        rl = AR.f32(512)
        b_rl = Buf()
        ost = [AR.bf16(512) for _ in range(2)]
        b_ost = [Buf() for _ in range(2)]
        s_ost = [new_slot() for _ in range(2)]
        accD = [AR.f32(512) for _ in range(2)]
        accP = [AR.f32(512) for _ in range(2)]
        b_accD = [Buf() for _ in range(2)]
        b_accP = [Buf() for _ in range(2)]
        ones_f = AR.f32(128)
        b_onesf = Buf()
        P.op("dve", "memset", ones_f, 1.0, writes=[b_onesf])
        NS = NKEY // 128
        SCALE = 128.0 ** -0.5
        STB = [0, 1, 2, 7]
        qi = 0
        pending = [None]
        for h in range(8):
            kvh = h // 4
            for qb in range(8):
                qs = qi % 2
                P.dma("sp", s_qblk[qs], qblk[qs], qT_s[h, :, qb * 512:(qb + 1) * 512], writes=[b_qblk[qs]])
                ob = 3 + (qi % 2)
                lb = 5 + (qi % 2)

                def mm1(s_):
                    P.op("pe", "matmul", PS(STB[s_ % 4]), kT_sb[:, kvh, s_ * 128:(s_ + 1) * 128], qblk[qs],
                         start=True, stop=True, reads=[b_kv, b_qblk[qs]], writes=[pbank[STB[s_ % 4]]])
                mm1(0)
                mm1(1)
                for s_ in range(NS):
                    if s_ + 2 < NS:
                        mm1(s_ + 2)
                    pi = s_ % NPT
                    P.op("act", "activation", out=PT[pi], in_=PS(STB[s_ % 4]), func=AF.Exp, scale=SCALE,
                         reads=[pbank[STB[s_ % 4]]], writes=[b_PT[pi]])
                    P.op("pe", "matmul", PS(ob), v_sb[:, s_, kvh * 128:(kvh + 1) * 128], PT[pi],
                         start=(s_ == 0), stop=(s_ == NS - 1), reads=[b_kv, b_PT[pi]], writes=[pbank[ob]])
                    if s_ % 3 == 2:
                        eng, acc, bacc, first = "pool", accP[qs], b_accP[qs], (s_ == 2)
                    else:
                        eng, acc, bacc, first = "dve", accD[qs], b_accD[qs], (s_ == 0)
                    if first:
                        P.op(eng, "tensor_copy", out=acc, in_=PT[pi], reads=[b_PT[pi]], writes=[bacc])
                    else:
                        P.op(eng, "tensor_tensor", out=acc, in0=acc, in1=PT[pi], op=ALU.add,
                             reads=[b_PT[pi], bacc], writes=[bacc])
                    if s_ == 3 and pending[0] is not None:
                        pending[0]()
                        pending[0] = None

                def finish(h=h, qb=qb, qs=qs, ob=ob, lb=lb):
                    P.op("pe", "matmul", PS(lb), ones_f, accD[qs], start=True, stop=False,
                         reads=[b_onesf, b_accD[qs]], writes=[pbank[lb]])
                    P.op("pe", "matmul", PS(lb), ones_f, accP[qs], start=False, stop=True,
                         reads=[b_onesf, b_accP[qs]], writes=[pbank[lb]])
                    P.op("dve", "reciprocal", out=rl, in_=PS(lb), reads=[pbank[lb]], writes=[b_rl])
                    P.op("dve", "tensor_tensor", out=ost[qs], in0=PS(ob), in1=rl, op=ALU.mult,
                         reads=[pbank[ob], b_rl], writes=[b_ost[qs]])
                    P.dma("pool", s_ost[qs], yaT_s[h, :, qb * 512:(qb + 1) * 512], ost[qs],
                          reads=[b_ost[qs]], writes=[b_scr])
                pending[0] = finish
                qi += 1
        pending[0]()

        P.barrier()
        AR.reset(persist_off)
        rc = AR.f32(1024)
        b_rc = Buf()
        s_rc = new_slot()
        P.dma("sp", new_slot(), rc, rconst[:, :], writes=[b_rc])
        D1, D2 = rc[:, 0:128], rc[:, 128:256]
        L1, L2 = rc[:, 256:384], rc[:, 384:512]
        I1, I2 = rc[:, 512:640], rc[:, 640:768]
        E_f, E_b = rc[:, 768:800], rc[:, 800:832]
        c127, cp = rc[:, 832:833], rc[:, 833:834]
        lg = AR.f32(16)
        b_lg = Buf()
        P.dma("sp", new_slot(), lg, declg[:, :], writes=[b_lg])
        P.op("act", "activation", out=lg, in_=lg, func=AF.Sigmoid, reads=[b_lg], writes=[b_lg])
        P.op("act", "activation", out=lg, in_=lg, func=AF.Ln, reads=[b_lg], writes=[b_lg])
        tabs = AR.f32(64)
        b_tabs = Buf()
        gf127, gbr, g128 = tabs[:, 0:8], tabs[:, 8:16], tabs[:, 16:32]
        a_f, a_b = tabs[:, 32:40], tabs[:, 40:48]
        P.op("act", "activation", out=gf127, in_=lg[:, 0:8], func=AF.Exp, scale=c127, reads=[b_lg, b_rc], writes=[b_tabs])
        P.op("act", "activation", out=gbr, in_=lg[:, 8:16], func=AF.Exp, scale=cp, reads=[b_lg, b_rc], writes=[b_tabs])
        P.op("act", "activation", out=g128, in_=lg, func=AF.Exp, scale=128.0, reads=[b_lg], writes=[b_tabs])
        P.op("dve", "tensor_scalar", out=tabs[:, 48:49], in0=hf_t, scalar1=4096.0, scalar2=None, op0=ALU.mult,
             reads=[b_hf], writes=[b_tabs])
        P.op("dve", "tensor_scalar", out=tabs[:, 49:50], in0=hf_t, scalar1=-1.0, scalar2=1.0, op0=ALU.mult, op1=ALU.add,
             reads=[b_hf], writes=[b_tabs])
        P.op("dve", "tensor_scalar", out=tabs[:, 50:51], in0=tabs[:, 49:50], scalar1=4096.0, scalar2=None, op0=ALU.mult,
             reads=[b_tabs], writes=[b_tabs])
        omhf = tabs[:, 49:50]
        P.op("act", "activation", out=a_f, in_=lg[:, 0:8], func=AF.Exp, scale=tabs[:, 48:49], reads=[b_lg, b_tabs], writes=[b_tabs])
        P.op("act", "activation", out=a_b, in_=lg[:, 8:16], func=AF.Exp, scale=tabs[:, 50:51], reads=[b_lg, b_tabs], writes=[b_tabs])
        MT = AR.f32(8, 128)
        decF = AR.f32(8, 128)
        decB = AR.f32(8, 128)
        tm1 = AR.f32(128)
        tm2 = AR.f32(128)
        b_tm = Buf()
        b_MT = Buf()
        for h in range(8):
            P.op("act", "activation", out=tm1, in_=D1, func=AF.Exp, scale=lg[:, h:h + 1], reads=[b_lg, b_rc], writes=[b_tm])
            P.op("dve", "tensor_tensor", out=tm1, in0=tm1, in1=L1, op=ALU.mult, reads=[b_tm, b_rc], writes=[b_tm])
            P.op("act", "activation", out=tm2, in_=D2, func=AF.Exp, scale=lg[:, 8 + h:9 + h], reads=[b_lg, b_rc], writes=[b_tm])
            P.op("dve", "tensor_tensor", out=tm2, in0=tm2, in1=L2, op=ALU.mult, reads=[b_tm, b_rc], writes=[b_tm])
            P.op("dve", "tensor_tensor", out=MT[:, h], in0=tm1, in1=tm2, op=ALU.add, reads=[b_tm], writes=[b_MT])
            P.op("act", "activation", out=decF[:, h], in_=I1, func=AF.Exp, scale=lg[:, h:h + 1], reads=[b_lg, b_rc], writes=[b_MT])
            P.op("act", "activation", out=decB[:, h], in_=I2, func=AF.Exp, scale=lg[:, 8 + h:9 + h], reads=[b_lg, b_rc], writes=[b_MT])
        Wt = AR.f32(32, 8)
        Wt2 = AR.f32(32, 8)
        b_Wt = Buf()
        P.op("dve", "tensor_tensor", out=Wt, in0=bc(E_f.unsqueeze(2), [128, 32, 8]), in1=bc(lg[:, 0:8].unsqueeze(1), [128, 32, 8]), op=ALU.mult,
             reads=[b_lg, b_rc], writes=[b_Wt])
        P.op("act", "activation", out=Wt, in_=Wt, func=AF.Exp, reads=[b_Wt], writes=[b_Wt])
        P.op("dve", "tensor_tensor", out=Wt, in0=Wt, in1=bc(gf127.unsqueeze(1), [128, 32, 8]), op=ALU.mult, reads=[b_Wt, b_tabs], writes=[b_Wt])
        P.op("dve", "tensor_scalar", out=Wt, in0=Wt, scalar1=hf_t, scalar2=None, op0=ALU.mult, reads=[b_Wt, b_hf], writes=[b_Wt])
        P.op("dve", "tensor_tensor", out=Wt2, in0=bc(E_b.unsqueeze(2), [128, 32, 8]), in1=bc(lg[:, 8:16].unsqueeze(1), [128, 32, 8]), op=ALU.mult,
             reads=[b_lg, b_rc], writes=[b_Wt])
        P.op("act", "activation", out=Wt2, in_=Wt2, func=AF.Exp, reads=[b_Wt], writes=[b_Wt])
        P.op("dve", "tensor_tensor", out=Wt2, in0=Wt2, in1=bc(gbr.unsqueeze(1), [128, 32, 8]), op=ALU.mult, reads=[b_Wt, b_tabs], writes=[b_Wt])
        P.op("dve", "tensor_scalar", out=Wt2, in0=Wt2, scalar1=omhf, scalar2=None, op0=ALU.mult, reads=[b_Wt, b_tabs], writes=[b_Wt])
        P.op("dve", "tensor_tensor", out=Wt, in0=Wt, in1=Wt2, op=ALU.add, reads=[b_Wt], writes=[b_Wt])
        Wc = AR.f32(4, 8)
        P.op("dve", "tensor_tensor", out=Wc[:, 0], in0=gf127, in1=g128[:, 0:8], op=ALU.mult, reads=[b_tabs], writes=[b_Wt])
        P.op("dve", "tensor_copy", out=Wc[:, 1], in_=gf127, reads=[b_tabs], writes=[b_Wt])
        P.op("dve", "tensor_copy", out=Wc[:, 2], in_=gbr, reads=[b_tabs], writes=[b_Wt])
        P.op("dve", "tensor_tensor", out=Wc[:, 3], in0=gbr, in1=g128[:, 8:16], op=ALU.mult, reads=[b_tabs], writes=[b_Wt])

        NKV = 2
        krb = [AR.bf16(4, 1024) for _ in range(NKV)]
        vrb = [AR.bf16(4, 1024) for _ in range(NKV)]
        b_krb = [Buf() for _ in range(NKV)]
        s_krb = [new_slot() for _ in range(NKV)]
        kvi = [0]

        def load_krvr(row0, nt):
            i = kvi[0] % NKV
            kvi[0] += 1
            P.dma("sp", s_krb[i], krb[i][:, :nt], kr_s[row0:row0 + nt * 128, :].rearrange("(t p) n -> p t n", p=128), writes=[b_krb[i]])
            P.dma("sp", s_krb[i], vrb[i][:, :nt], vr_s[row0:row0 + nt * 128, :].rearrange("(t p) n -> p t n", p=128), writes=[b_krb[i]])
            return i

        krw = [AR.bf16(8, 128) for _ in range(2)]
        b_krw = [Buf() for _ in range(2)]
        krw_i = [0]

        def state_accum(i, tt, wap, banks, first, last, eng="dve"):
            wi = krw_i[0] % 2
            krw_i[0] += 1
            P.op(eng, "tensor_tensor", out=krw[wi], in0=krb[i][:, tt].rearrange("p (h d) -> p h d", h=8),
                                                in1=bc(wap.unsqueeze(2), [128, 8, 128]), op=ALU.mult,
                 reads=[b_krb[i], b_Wt, b_tabs], writes=[b_krw[wi]])
            for h in range(8):
                bk = banks[h // 4]
                P.op("pe", "matmul", PS(bk)[:, (h % 4) * 128:(h % 4 + 1) * 128], krw[wi][:, h, :],
                                                          vrb[i][:, tt, h * 128:(h + 1) * 128], start=first, stop=last,
                     reads=[b_krw[wi], b_krb[i]], writes=[pbank[bk]])

        Sst = AR.f32(8, 128)
        Tst = AR.f32(8, 128)
        b_Sst = Buf()
        b_Tst = Buf()
        accF = AR.f32(8, 128)
        accB = AR.f32(8, 128)
        accO = AR.f32(8, 128)
        b_acc = [Buf() for _ in range(3)]
        bankpairs = [(0, 1), (2, 3), (4, 5), (6, 7)]
        bp_i = [0]

        def accum_tile(i, tt, wap, acc, b_a, firstflag, eng="dve"):
            bp = bankpairs[bp_i[0] % 4]
            bp_i[0] += 1
            state_accum(i, tt, wap, bp, True, True, eng=eng)
            for g in range(2):
                if firstflag:
                    P.op("dve", "tensor_copy", out=acc[:, 4 * g:4 * g + 4], in_=PS(bp[g]).rearrange("p (h d) -> p h d", h=4),
                         reads=[pbank[bp[g]]], writes=[b_a])
                else:
                    P.op("dve", "tensor_tensor", out=acc[:, 4 * g:4 * g + 4], in0=acc[:, 4 * g:4 * g + 4],
                         in1=PS(bp[g]).rearrange("p (h d) -> p h d", h=4), op=ALU.add,
                         reads=[pbank[bp[g]], b_a], writes=[b_a])

        i = load_krvr(0, 2)
        for tt in range(2):
            accum_tile(i, tt, Wc[:, tt], accF, b_acc[0], tt == 0)
            accum_tile(i, tt, Wc[:, 2 + tt], accB, b_acc[1], tt == 0)
        for t4 in range(8):
            i = load_krvr(CTX + t4 * 512, 4)
            for tt in range(4):
                t = t4 * 4 + tt
                accum_tile(i, tt, Wt[:, t], accO, b_acc[2], t == 0, eng=("dve" if tt % 2 == 0 else "pool"))
        P.op("dve", "tensor_tensor", out=Sst, in0=accF, in1=bc(a_f.unsqueeze(2), [128, 8, 128]), op=ALU.mult,
             reads=[b_acc[0], b_tabs], writes=[b_Sst])
        P.op("dve", "tensor_tensor", out=Tst, in0=accB, in1=bc(a_b.unsqueeze(2), [128, 8, 128]), op=ALU.mult,
             reads=[b_acc[1], b_tabs], writes=[b_Tst])
        P.op("dve", "scalar_tensor_tensor", out=Sst, in0=accO, scalar=hf_t, in1=Sst, op0=ALU.mult, op1=ALU.add,
             reads=[b_acc[2], b_hf, b_Sst], writes=[b_Sst])
        P.op("dve", "scalar_tensor_tensor", out=Tst, in0=accO, scalar=omhf, in1=Tst, op0=ALU.mult, op1=ALU.add,
             reads=[b_acc[2], b_tabs, b_Tst], writes=[b_Tst])
        Tb = AR.bf16(32, 1024)
        b_Tb = Buf()
        OWN0 = CTX + HALF
        for c4 in range(7, -1, -1):
            i = load_krvr(OWN0 + c4 * 512, 4)
            for tt in range(3, -1, -1):
                c = c4 * 4 + tt
                P.op("act", "activation", out=Tb[:, c], in_=Tst.rearrange("p h d -> p (h d)"), func=AF.Copy,
                     reads=[b_Tst], writes=[b_Tb])
                if c == 0:
                    break
                state_accum(i, tt, gbr, (6, 7), True, True, eng="pool")
                P.op("pool", "tensor_tensor", out=Tst, in0=Tst, in1=bc(g128[:, 8:16].unsqueeze(2), [128, 8, 128]), op=ALU.mult,
                     reads=[b_Tst, b_tabs], writes=[b_Tst])
                for g in range(2):
                    P.op("dve", "tensor_tensor", out=Tst[:, 4 * g:4 * g + 4], in0=Tst[:, 4 * g:4 * g + 4],
                                                               in1=PS(6 + g).rearrange("p (h d) -> p h d", h=4), op=ALU.add,
                         reads=[pbank[6 + g], b_Tst], writes=[b_Tst])
        qrb = [AR.bf16(8, 512)] * 2
        ktb = [AR.bf16(8, 512)] * 2
        sgb_ = [AR.bf16(4, 1024)] * 2
        b_qrb = [Buf()] * 2
        s_qrb = [new_slot()] * 2
        Sb16 = AR.bf16(8, 128)
        b_Sb16 = Buf()
        qf = AR.bf16(8, 128)
        qbk = AR.bf16(8, 128)
        b_qf = Buf()
        Pm = AR.bf16(8, 128)
        b_Pm = Buf()
        ysq = AR.f32(1024)
        yn = AR.f32(1024)
        yg = AR.bf16(1024)
        b_y = Buf()
        yr_ = AR.f32(32)
        yst = [AR.bf16(1024) for _ in range(2)]
        b_yst = [Buf() for _ in range(2)]
        s_yst = [new_slot() for _ in range(2)]
        for c4 in range(8):
            i = load_krvr(OWN0 + c4 * 512, 4)
            j = c4 % 2
            P.dma("sp", s_qrb[j], qrb[j], qrT_s[:, :, c4 * 512:(c4 + 1) * 512].rearrange("h d t -> d h t"), writes=[b_qrb[j]])
            P.dma("sp", s_qrb[j], ktb[j], krT_s[:, :, c4 * 512:(c4 + 1) * 512].rearrange("h d t -> d h t"), writes=[b_qrb[j]])
            P.dma("sp", s_qrb[j], sgb_[j], sgr_s[c4 * 512:(c4 + 1) * 512, :].rearrange("(t p) n -> p t n", p=128), writes=[b_qrb[j]])
            for tt in range(4):
                c = c4 * 4 + tt
                ts = slice(tt * 128, (tt + 1) * 128)
                P.op("act", "activation", out=Sb16.rearrange("p h d -> p (h d)"), in_=Sst.rearrange("p h d -> p (h d)"), func=AF.Copy,
                     reads=[b_Sst], writes=[b_Sb16])
                P.op("dve", "tensor_tensor", out=qf, in0=qrb[j][:, :, ts], in1=decF, op=ALU.mult,
                     reads=[b_qrb[j], b_MT], writes=[b_qf])
                P.op("pool", "tensor_tensor", out=qbk, in0=qrb[j][:, :, ts], in1=decB, op=ALU.mult,
                     reads=[b_qrb[j], b_MT], writes=[b_qf])
                for h in range(8):
                    bk = h // 4
                    P.op("pe", "matmul", PS(bk)[:, (h % 4) * 128:(h % 4 + 1) * 128], ktb[j][:, h, ts],
                                                                    qrb[j][:, h, ts], start=True, stop=True,
                         reads=[b_qrb[j]], writes=[pbank[bk]])
                for g in range(2):
                    P.op("dve", "tensor_tensor", out=Pm[:, 4 * g:4 * g + 4], in0=PS(g).rearrange("p (h d) -> p h d", h=4),
                                                               in1=MT[:, 4 * g:4 * g + 4], op=ALU.mult,
                         reads=[pbank[g], b_MT], writes=[b_Pm])
                for h in range(8):
                    bk = 2 + h // 4
                    osl = slice((h % 4) * 128, (h % 4 + 1) * 128)
                    P.op("pe", "matmul", PS(bk)[:, osl], Pm[:, h, :], vrb[i][:, tt, h * 128:(h + 1) * 128],
                                                                      start=True, stop=False,
                         reads=[b_Pm, b_krb[i]], writes=[pbank[bk]])
                    P.op("pe", "matmul", PS(bk)[:, osl], qf[:, h, :], Sb16[:, h, :], start=False, stop=False,
                         reads=[b_qf, b_Sb16], writes=[pbank[bk]])
                    P.op("pe", "matmul", PS(bk)[:, osl], qbk[:, h, :], Tb[:, c, h * 128:(h + 1) * 128],
                                                                           start=False, stop=True,
                         reads=[b_qf, b_Tb], writes=[pbank[bk]])
                for g in range(2):
                    P.op("act", "activation", out=ysq[:, g * 512:(g + 1) * 512], in_=PS(2 + g), func=AF.Square,
                         reads=[pbank[2 + g]], writes=[b_y])
                P.op("dve", "tensor_reduce", out=yr_[:, 0:8], in_=ysq.rearrange("p (h d) -> p h d", h=8), axis=AX.X, op=ALU.add,
                     reads=[b_y], writes=[b_y])
                P.op("act", "activation", out=yr_[:, 8:16], in_=yr_[:, 0:8], func=AF.Sqrt, scale=1.0 / 128, bias=EPS,
                     reads=[b_y], writes=[b_y])
                P.op("dve", "reciprocal", out=yr_[:, 16:24], in_=yr_[:, 8:16], reads=[b_y], writes=[b_y])
                for g in range(2):
                    P.op("dve", "tensor_tensor", out=yn[:, g * 512:(g + 1) * 512].rearrange("p (h d) -> p h d", h=4),
                                                               in0=PS(2 + g).rearrange("p (h d) -> p h d", h=4),
                                                               in1=bc(yr_[:, 16 + 4 * g:20 + 4 * g].unsqueeze(2), [128, 4, 128]), op=ALU.mult,
                         reads=[pbank[2 + g], b_y], writes=[b_y])
                P.op("pool", "tensor_tensor", out=yg, in0=yn, in1=sgb_[j][:, tt], op=ALU.mult,
                     reads=[b_y, b_qrb[j]], writes=[b_y])
                for h in range(8):
                    P.op("pe", "transpose", PSB(4)[:, h * 128:(h + 1) * 128], yg[:, h * 128:(h + 1) * 128], ident,
                         reads=[b_y, b_ident], writes=[pbank[4]])
                yi = c % 2
                P.op("act", "activation", out=yst[yi], in_=PSB(4), func=AF.Copy, reads=[pbank[4]], writes=[b_yst[yi]])
                P.dma(STQ, s_yst[yi], yrT_s[:, :, c * 128:(c + 1) * 128].rearrange("h e t -> e h t"),
                      yst[yi].rearrange("p (h t) -> p h t", h=8), reads=[b_yst[yi]], writes=[b_scr])
                if c < 31:
                    state_accum(i, tt, gf127, (6, 7), True, True, eng="pool")
                    P.op("pool", "tensor_tensor", out=Sst, in0=Sst, in1=bc(g128[:, 0:8].unsqueeze(2), [128, 8, 128]), op=ALU.mult,
                         reads=[b_Sst, b_tabs, b_Sb16], writes=[b_Sst])
                    for g in range(2):
                        P.op("dve", "tensor_tensor", out=Sst[:, 4 * g:4 * g + 4], in0=Sst[:, 4 * g:4 * g + 4],
                                                                   in1=PS(6 + g).rearrange("p (h d) -> p h d", h=4), op=ALU.add,
                             reads=[pbank[6 + g], b_Sst], writes=[b_Sst])

        P.barrier()
        AR.reset(persist_off)
        alloc_common()
        nT3 = AR.bf16(16, 512)
        b_nT3 = Buf()
        uT = nT3
        b_uT = b_nT3
        yrb = AR.bf16(8, 512)
        b_yrb = Buf()
        s_yrb = new_slot()
        fn_t = AR.f32(D)
        b_fn = Buf()
        P.dma("sp", new_slot(), fn_t, fnorm[:, :], writes=[b_fn])
        mt1 = AR.f32(512)
        mt2 = AR.f32(512)
        b_mt = Buf()
        s_gt = new_slot()
        s_out = [new_slot() for _ in range(4)]
        gflat = C.gT.rearrange("p f t -> p (f t)")
        sga = gflat[:, 0:8192].rearrange("p (k t) -> p k t", k=16)
        sgbb = gflat[:, 8192:16384].rearrange("p (k t) -> p k t", k=16)
        yab = gflat[:, 16384:20480].rearrange("p (k t) -> p k t", k=8)
        ada_gate(5, [(0, 0, 1.0)])
        ada_mod(6, 4, False)
        ada_mod(7, 5, True)
        ada_gate(8, [(0, 1, 0.5)])
        for bI in range(8):
            t0 = bI * 512
            for t in range(4):
                P.dma("sp", C.s_X[t], C.X[:, t], h1_s[t0 + t * 128: t0 + (t + 1) * 128, :], reads=[b_scr], writes=[C.b_X[t]])
            P.dma("sp", s_gt, sga, sgaT_s[:, :, t0:t0 + 512].rearrange("k d t -> d k t"), reads=[b_scr], writes=[C.b_gT])
            P.dma("sp", s_gt, sgbb, sgbT_s[:, :, t0:t0 + 512].rearrange("k d t -> d k t"), reads=[b_scr], writes=[C.b_gT])
            P.dma("sp", s_gt, yab, yaT_s[:, :, t0:t0 + 512].rearrange("k d t -> d k t"), reads=[b_scr], writes=[C.b_gT])
            P.dma("sp", s_yrb, yrb, yrT_s[:, :, t0:t0 + 512].rearrange("k d t -> d k t"), reads=[b_scr], writes=[b_yrb])
            for u in range(4):
                i = next_wslot()
                P.dma("sp", C.s_wring[i], C.wring[i][:, 0:8, :], wpa_b[:, u * 512:(u + 1) * 512].rearrange("(kc p) n -> p kc n", p=128),
                      reads=[b_wpa], writes=[C.b_wring[i]])
                P.dma("sp", C.s_wring[i], C.wring[i][:, 8:16, :], wpr_b[:, u * 512:(u + 1) * 512].rearrange("(kc p) n -> p kc n", p=128),
                      reads=[b_wpr], writes=[C.b_wring[i]])
                for c4 in range(4):
                    dmc = u * 4 + c4
                    ba, bb = (0, 1) if dmc % 2 == 0 else (2, 3)
                    for kc in range(8):
                        P.op("pe", "matmul", PS(ba), C.wring[i][:, kc, c4 * 128:(c4 + 1) * 128],
                                                                              yab[:, kc, :], start=(kc == 0), stop=(kc == 7),
                             reads=[C.b_wring[i], C.b_gT], writes=[pbank[ba]])
                    for kc in range(8):
                        P.op("pe", "matmul", PS(bb), C.wring[i][:, 8 + kc, c4 * 128:(c4 + 1) * 128],
                                                                              yrb[:, kc, :], start=(kc == 0), stop=(kc == 7),
                             reads=[C.b_wring[i], b_yrb], writes=[pbank[bb]])
                    P.op("dve", "tensor_tensor", out=mt1, in0=PS(ba), in1=sga[:, dmc, :], op=ALU.mult,
                         reads=[pbank[ba], C.b_gT], writes=[b_mt])
                    P.op("dve", "tensor_tensor", out=mt2, in0=PS(bb), in1=sgbb[:, dmc, :], op=ALU.mult,
                         reads=[pbank[bb], C.b_gT], writes=[b_mt])
                    P.op("pool", "tensor_tensor", out=uT[:, dmc, :], in0=mt1, in1=mt2, op=ALU.add,
                         reads=[b_mt], writes=[b_uT])
            out_gemm(4, lambda k, t: uT[:, k, t * 128:(t + 1) * 128], [b_uT], 16, wmo_b, b_wmo, 0)
            norm_T(4, 2, 0, nT3, b_nT3)
            ffn(4, nT3, b_nT3, w2i_b, b_w2i, w2o_b, b_w2o, 1)
            for t in range(4):
                P.op("act", "activation", out=C.xn[0], in_=C.X[:, t], func=AF.Square, accum_out=C.ss[:, 0:1],
                     reads=[C.b_X[t]], writes=[C.b_ss, C.b_xn[0]])
                P.op("act", "activation", out=C.ss[:, 1:2], in_=C.ss[:, 0:1], func=AF.Sqrt, scale=1.0 / D, bias=EPS,
                     reads=[C.b_ss], writes=[C.b_ss])
                P.op("dve", "reciprocal", out=C.ss[:, 2:3], in_=C.ss[:, 1:2], reads=[C.b_ss], writes=[C.b_ss])
                P.op("dve", "scalar_tensor_tensor", out=C.X[:, t], in0=C.X[:, t], scalar=C.ss[:, 2:3], in1=fn_t,
                                                                  op0=ALU.mult, op1=ALU.mult,
                     reads=[C.b_X[t], C.b_ss, b_fn], writes=[C.b_X[t]])
                P.dma("pool", s_out[t], out[t0 + t * 128: t0 + (t + 1) * 128, :], C.X[:, t], reads=[C.b_X[t]], writes=[b_scr])
        P.barrier()
        P.finalize(block, engsem)
    return nc


def _rope_tables():
    rows = SEQ // 64
    row = np.repeat(np.arange(rows, dtype=np.float32), 64)
    col = np.tile(np.arange(64, dtype=np.float32), rows)
    half = 64
    inv_freq = (10000.0 ** (-np.arange(0, half, 2, dtype=np.float32) / half)).astype(np.float32)
    ang = np.concatenate([row[:, None] * inv_freq, col[:, None] * inv_freq], axis=-1).astype(np.float32)
    return np.cos(ang).astype(np.float32), np.sin(ang).astype(np.float32)


def _rconst():
    rcn = np.zeros((128, 1024), np.float32)
    p = np.arange(128, dtype=np.float32)
    j = p[:, None]
    i = p[None, :]
    rcn[:, 0:128] = np.maximum(i - j, 0)
    rcn[:, 128:256] = np.maximum(j - i, 0)
    rcn[:, 256:384] = (i >= j)
    rcn[:, 384:512] = (j >= i)
    rcn[:, 512:640] = i + 1
    rcn[:, 640:768] = 128 - i
    t = np.arange(32, dtype=np.float32)
    rcn[:, 768:800] = 128 * (31 - t)[None, :]
    rcn[:, 800:832] = 128 * t[None, :]
    rcn[:, 832] = 127 - p
    rcn[:, 833] = p
    return rcn


_NC_CACHE = {}


def kernel(x, c, ctx, c_ctx, w_ada, b_ada, ffn1_w_in, ffn1_w_out, mix_w_in, attn_q_gain, attn_k_gain,
           ret_decay_logit, w_proj_attn, w_proj_ret, mix_w_out, ffn2_w_in, ffn2_w_out, final_norm):
    f = lambda a: np.ascontiguousarray(np.asarray(a, dtype=np.float32))
    x, c, ctx, c_ctx = f(x), f(c), f(ctx), f(c_ctx)
    if "nc" not in _NC_CACHE:
        _NC_CACHE["nc"] = build_program()
    nc = _NC_CACHE["nc"]
    cos, sin = _rope_tables()
    rope_x = np.concatenate([cos, sin], axis=1)
    rope_ctx = np.concatenate([np.ones((CTX, 64), np.float32), np.zeros((CTX, 64), np.float32)], axis=1)
    shared = {
        "b_adaT": f(np.asarray(b_ada)[0].reshape(144, 128).T),
        "b_ada": f(np.asarray(b_ada)[0][None, :]),
        "w_ada": f(np.asarray(w_ada)[0]),
        "ffn1_w_in": f(np.asarray(ffn1_w_in)[0]),
        "ffn1_w_out": f(np.asarray(ffn1_w_out)[0]),
        "mix_w_in": f(np.asarray(mix_w_in)[0]),
        "gains": f(np.tile(np.concatenate([np.asarray(attn_q_gain)[0], np.asarray(attn_k_gain)[0]])[None, :], (128, 1))),
        "declg": f(np.tile(np.asarray(ret_decay_logit)[0].reshape(1, 16), (128, 1))),
        "w_proj_attn": f(np.asarray(w_proj_attn)[0]),
        "w_proj_ret": f(np.asarray(w_proj_ret)[0]),
        "mix_w_out": f(np.asarray(mix_w_out)[0]),
        "ffn2_w_in": f(np.asarray(ffn2_w_in)[0]),
        "ffn2_w_out": f(np.asarray(ffn2_w_out)[0]),
        "fnorm": f(np.tile(np.asarray(final_norm)[None, :], (128, 1))),
        "rconst": _rconst(),
    }
    in_maps = []
    for core in range(8):
        b, hf = core // 2, core % 2
        own = slice(hf * HALF, (hf + 1) * HALF)
        oth = slice((1 - hf) * HALF, (2 - hf) * HALF)
        m = dict(shared)
        m["x_all"] = np.concatenate([ctx[b], x[b, oth], x[b, own]], axis=0)
        m["rope"] = np.concatenate([rope_ctx, rope_x[oth], rope_x[own]], axis=0)
        cc = np.stack([c[b].reshape(16, 128).T, c_ctx.reshape(16, 128).T], axis=-1)
        m["cT"] = f(cc.reshape(128, 32))
        m["hf"] = np.full((128, 1), float(hf), np.float32)
        in_maps.append(m)
    res = run_bass_kernel_spmd(nc, in_maps, core_ids=list(range(8)))
    outp = np.empty((4, SEQ, D), np.float32)
    for core in range(8):
        b, hf = core // 2, core % 2
        outp[b, hf * HALF:(hf + 1) * HALF] = res.results[core]["out"]
    return outp
```

```python
import contextlib
import numpy as np
import concourse.bass as bass
import concourse.mybir as mybir
from concourse.bass_utils import run_bass_kernel_spmd

F32 = mybir.dt.float32
BF16 = mybir.dt.bfloat16
AF = mybir.ActivationFunctionType
ALU = mybir.AluOpType
AX = mybir.AxisListType

D = 2048
DFF = 5632
NF = DFF // 128
SEQ = 8192
HALF = 4096
CTX = 256
NKEY = CTX + SEQ
PW = 9728
EPS = 1e-6
ENGS = ("pe", "act", "dve", "pool", "sp")
_FL = ["INTER"]
STQ = "act" if "ACTQ" in _FL else "pool"


class Rec:
    __slots__ = ("eng", "fn", "deps", "signal", "value", "is_dma", "sem", "idx")


class Buf:
    __slots__ = ("name", "writers", "readers", "prev")

    def __init__(self, name=""):
        self.name = name
        self.writers = []
        self.readers = []
        self.prev = []


class DmaSlot:
    __slots__ = ("sem", "count")

    def __init__(self, sem):
        self.sem = sem
        self.count = 0


def _compress(lst):
    best = {}
    for r in lst:
        key = ("d", id(r.sem)) if r.is_dma else ("e", r.eng)
        o = best.get(key)
        if o is None:
            best[key] = r
        elif r.is_dma:
            if r.value > o.value:
                best[key] = r
        elif r.idx > o.idx:
            best[key] = r
    return list(best.values())


class Prog:
    def __init__(self, nc):
        self.nc = nc
        self.q = {e: [] for e in ENGS}

    def _add(self, eng, fn, reads, writes, extra):
        r = Rec()
        r.eng = eng
        r.fn = fn
        r.signal = False
        r.value = 0
        r.is_dma = False
        r.sem = None
        r.idx = len(self.q[eng])
        deps = list(extra) if extra else []
        for b in reads:
            deps.extend(b.writers)
        for b in writes:
            if b.readers:
                b.prev = _compress(b.readers + b.writers)
                b.readers = []
                b.writers = []
            deps.extend(b.prev)
        r.deps = deps
        self.q[eng].append(r)
        return r

    def _post(self, r, reads, writes):
        for b in reads:
            b.readers.append(r)
            if len(b.readers) > 8:
                b.readers = _compress(b.readers)
        for b in writes:
            b.writers.append(r)
            if len(b.writers) > 8:
                b.writers = _compress(b.writers)

    def op(self, eng, name, *args, reads=(), writes=(), extra=None, **kw):
        def fn(e, name=name, args=args, kw=kw):
            return getattr(e, name)(*args, **kw)
        r = self._add(eng, fn, reads, writes, extra)
        self._post(r, reads, writes)
        return r

    def dma(self, eng, slot, out, in_, reads=(), writes=(), extra=None):
        def fn(e, out=out, in_=in_):
            return e.dma_start(out=out, in_=in_)
        r = self._add(eng, fn, reads, writes, extra)
        r.is_dma = True
        r.sem = slot.sem
        slot.count += 16
        r.value = slot.count
        self._post(r, reads, writes)
        return r

    def barrier(self):
        marks = []
        for e in ENGS:
            for r in reversed(self.q[e]):
                if not r.is_dma and r.fn is not None:
                    marks.append(r)
                    break
        for e in ENGS:
            best = {}
            for r in self.q[e]:
                if r.is_dma:
                    o = best.get(id(r.sem))
                    if o is None or r.value > o.value:
                        best[id(r.sem)] = r
            marks.extend(best.values())
        for e in ENGS:
            r = Rec()
            r.eng = e
            r.fn = None
            r.signal = False
            r.value = 0
            r.is_dma = False
            r.sem = None
            r.idx = len(self.q[e])
            r.deps = list(marks)
            self.q[e].append(r)

    def finalize(self, block, engsem):
        for e in ENGS:
            for r in self.q[e]:
                for d in r.deps:
                    if not d.is_dma:
                        d.signal = True
        for e in ENGS:
            cnt = 0
            for r in self.q[e]:
                if r.is_dma or r.fn is None:
                    continue
                if r.signal:
                    cnt += 1
                    r.value = cnt
                    r.sem = engsem[e]
        for e in ENGS:
            recs = self.q[e]

            def body(eng, recs=recs):
                waited = {}
                for r in recs:
                    need = {}
                    for d in r.deps:
                        k = id(d.sem)
                        o = need.get(k)
                        if o is None or o[1] < d.value:
                            need[k] = (d.sem, d.value)
                    for k, (sem_, val_) in need.items():
                        if waited.get(k, 0) < val_:
                            eng.wait_ge(sem_, val_)
                            waited[k] = val_
                    if r.fn is None:
                        continue
                    ins = r.fn(eng)
                    if r.is_dma:
                        ins.then_inc(r.sem, 16)
                    elif r.signal:
                        ins.then_inc(r.sem, 1)

            if not recs:
                continue
            {"pe": block.tensor, "act": block.scalar, "dve": block.vector,
             "pool": block.gpsimd, "sp": block.sync}[e](body)


class Arena:
    def __init__(self, ap, nwords):
        self.ap = ap
        self.n = nwords
        self.off = 0

    def reset(self, off=0):
        self.off = off

    def f32(self, *shape):
        n = int(np.prod(shape))
        assert self.off + n <= self.n, ("arena overflow", self.off, n, self.n)
        v = self.ap[:, self.off:self.off + n]
        self.off += n
        return self._shape(v, shape)

    def bf16(self, *shape):
        n = int(np.prod(shape))
        w = (n + 1) // 2
        assert self.off + w <= self.n, ("arena overflow", self.off, w, self.n)
        v = self.ap[:, self.off:self.off + w].bitcast(BF16)[:, 0:n]
        self.off += w
        return self._shape(v, shape)

    @staticmethod
    def _shape(v, shape):
        if len(shape) == 1:
            return v
        if len(shape) == 2:
            return v.rearrange("p (a b) -> p a b", a=shape[0])
        if len(shape) == 3:
            return v.rearrange("p (a b c) -> p a b c", a=shape[0], b=shape[1])
        raise ValueError(shape)


def bc(ap, shape):
    return ap.to_broadcast(list(shape))


def build_program():
    nc = bass.Bass("TRN2", target_bir_lowering=False)

    def din(name, shape):
        return nc.dram_tensor(name, list(shape), F32, kind="ExternalInput").ap()

    x_all = din("x_all", [NKEY, D])
    rope = din("rope", [NKEY, 128])
    cT = din("cT", [128, 32])
    b_adaT = din("b_adaT", [128, 144])
    b_ada = din("b_ada", [1, 9 * D])
    w_ada = din("w_ada", [D, 9 * D])
    w1i = din("ffn1_w_in", [D, 2 * DFF])
    w1o = din("ffn1_w_out", [DFF, D])
    wmix = din("mix_w_in", [D, PW])
    gains = din("gains", [128, 256])
    declg = din("declg", [128, 16])
    wpa = din("w_proj_attn", [1024, D])
    wpr = din("w_proj_ret", [1024, D])
    wmo = din("mix_w_out", [D, D])
    w2i = din("ffn2_w_in", [D, 2 * DFF])
    w2o = din("ffn2_w_out", [DFF, D])
    fnorm = din("fnorm", [128, D])
    rconst = din("rconst", [128, 1024])
    hfin = din("hf", [128, 1])
    out = nc.dram_tensor("out", [HALF, D], F32, kind="ExternalOutput").ap()

    def dscr(name, shape, dt=BF16):
        return nc.dram_tensor(name, list(shape), dt).ap()

    w1i_b = dscr("w1i_b", [D, 2 * DFF])
    w1o_b = dscr("w1o_b", [DFF, D])
    wmix_b = dscr("wmix_b", [D, PW])
    wpa_b = dscr("wpa_b", [1024, D])
    wpr_b = dscr("wpr_b", [1024, D])
    wmo_b = dscr("wmo_b", [D, D])
    w2i_b = dscr("w2i_b", [D, 2 * DFF])
    w2o_b = dscr("w2o_b", [DFF, D])
    h1_s = dscr("h1_s", [HALF, D], F32)
    qT_s = dscr("qT_s", [8, 128, HALF])
    kT_s = dscr("kT_s", [2, 128, NKEY])
    v_s = dscr("v_s", [NKEY, 256])
    qrT_s = dscr("qrT_s", [8, 128, HALF])
    krT_s = dscr("krT_s", [8, 128, HALF])
    kr_s = dscr("kr_s", [NKEY, 1024])
    vr_s = dscr("vr_s", [NKEY, 1024])
    sgr_s = dscr("sgr_s", [HALF, 1024])
    sgaT_s = dscr("sgaT_s", [16, 128, HALF])
    sgbT_s = dscr("sgbT_s", [16, 128, HALF])
    yaT_s = dscr("yaT_s", [8, 128, HALF])
    yrT_s = dscr("yrT_s", [8, 128, HALF])

    NW = 52000
    with contextlib.ExitStack() as st:
        arena_t = st.enter_context(nc.sbuf_tensor("arena", [128, NW], F32))
        AR = Arena(arena_t, NW)
        psb = [st.enter_context(nc.psum_tensor(f"psb{i}", [128, 512], F32)) for i in range(8)]
        engsem = {e: st.enter_context(nc.semaphore(f"es_{e}")) for e in ENGS}
        nslots = [0]

        def new_slot():
            nslots[0] += 1
            return DmaSlot(st.enter_context(nc.semaphore(f"ds{nslots[0]}")))

        block = st.enter_context(nc.Block())
        P = Prog(nc)
        pbank = [Buf(f"bank{i}") for i in range(8)]

        def PS(i):
            return psb[i][:, :]

        def PSB(i):
            return psb[i][:, :].bitcast(BF16)

        ident = AR.bf16(128)
        ones_b = AR.bf16(128)
        identf = AR.f32(128)
        b_ident = Buf()
        P.op("pool", "memset", identf, 0.0, writes=[b_ident])
        P.op("pool", "affine_select", out=identf, in_=identf, pattern=[[-1, 128]],
                                               compare_op=ALU.not_equal, fill=1.0, base=0,
                                               channel_multiplier=1, reads=[b_ident], writes=[b_ident])
        P.op("dve", "tensor_copy", out=ident, in_=identf, reads=[b_ident], writes=[b_ident])
        P.op("dve", "memset", ones_b, 1.0, writes=[b_ident])
        modT = AR.f32(6, 16, 2)
        b_modT = [Buf() for _ in range(6)]
        badT = AR.f32(144)
        b_badT = Buf()
        s_misc = new_slot()
        P.dma("sp", new_slot(), badT, b_adaT[:, :], writes=[b_badT])
        sc32 = AR.f32(32)
        scb = AR.bf16(16, 2)
        screp = AR.bf16(2, 16, 128)
        b_sc = Buf()
        P.dma("sp", new_slot(), sc32, cT[:, :], writes=[b_sc])
        sig32 = AR.f32(32)
        P.op("act", "activation", out=sig32, in_=sc32, func=AF.Silu, reads=[b_sc], writes=[b_sc])
        P.op("dve", "tensor_copy", out=scb, in_=sig32.rearrange("p (k w) -> p k w", w=2),
             reads=[b_sc], writes=[b_sc])
        for w in range(2):
            P.op("dve", "tensor_copy",
                out=screp[:, w], in_=bc(sig32.rearrange("p (k w) -> p k w", w=2)[:, :, w:w + 1], [128, 16, 128]),
                reads=[b_sc], writes=[b_sc])
        hf_t = AR.f32(1)
        b_hf = Buf()
        P.dma("sp", new_slot(), hf_t, hfin[:, :], writes=[b_hf])
        persist_off = AR.off

        def prepass(src, dst, rows, bufobj):
            sl = new_slot()
            for r0 in range(0, rows, 128):
                P.dma("pool", sl, dst[r0:r0 + 128, :], src[r0:r0 + 128, :], writes=[bufobj])

        b_w1i, b_w1o, b_wmix, b_wpa, b_wpr, b_wmo, b_w2i, b_w2o = [Buf() for _ in range(8)]
        NRING = 2
        GT_WORDS = NF * 512 // 2

        class Ctx:
            pass

        C = Ctx()

        def alloc_common():
            C.wring = [AR.bf16(16, 512) for _ in range(NRING)]
            C.b_wring = [Buf() for _ in range(NRING)]
            C.s_wring = [new_slot() for _ in range(NRING)]
            C.wring_i = 0
            C.woring = [AR.bf16(4, 512) for _ in range(3)]
            C.b_woring = [Buf() for _ in range(3)]
            C.s_woring = [new_slot() for _ in range(3)]
            C.woring_i = 0
            C.X = AR.f32(4, D)
            C.b_X = [Buf() for _ in range(4)]
            C.s_X = [new_slot() for _ in range(4)]
            C.xn = [AR.bf16(D) for _ in range(2)]
            C.b_xn = [Buf() for _ in range(2)]
            C.xn_i = 0
            C.ss = AR.f32(8)
            C.b_ss = Buf()
            C.b_ss2 = [Buf(), Buf()]
            C.gT = AR.bf16(NF, 512)
            C.b_gT = Buf()
            C.silu = [AR.f32(512) for _ in range(2)]
            C.b_silu = [Buf() for _ in range(2)]
            C.silu_i = 0
            C.G = [AR.f32(D) for _ in range(2)]
            C.b_G = [Buf() for _ in range(2)]
            C.tmpe = [AR.f32(512) for _ in range(2)]
            C.b_tmpe = [Buf() for _ in range(2)]
            C.tmpe_i = 0
            C.brow = AR.bf16(512)
            C.b_brow = Buf()
            C.s_brow = new_slot()

        def next_wslot():
            i = C.wring_i % NRING
            C.wring_i += 1
            return i

        def ada_mod(m, mi, plus1):
            bank = 3
            for qd in range(4):
                i = next_wslot()
                P.dma("pool", C.s_wring[i], C.wring[i],
                      w_ada[:, m * D + qd * 512: m * D + (qd + 1) * 512].rearrange("(kc p) n -> p kc n", p=128),
                      writes=[C.b_wring[i]])
                for c4 in range(4):
                    nci = qd * 4 + c4
                    for kc in range(16):
                        P.op("pe", "matmul",
                            PS(bank)[:, nci * 2:nci * 2 + 2], C.wring[i][:, kc, c4 * 128:(c4 + 1) * 128],
                            scb[:, kc, :], start=(kc == 0), stop=(kc == 15),
                            reads=[C.b_wring[i], b_sc], writes=[pbank[bank]])
            P.op("dve", "tensor_tensor",
                out=modT[:, mi], in0=PS(bank)[:, 0:32].rearrange("p (n w) -> p n w", w=2),
                in1=bc(badT[:, m * 16:(m + 1) * 16].unsqueeze(2), [128, 16, 2]), op=ALU.add,
                reads=[pbank[bank], b_badT], writes=[b_modT[mi]])
            if plus1:
                P.op("dve", "tensor_scalar_add", modT[:, mi], modT[:, mi], 1.0,
                     reads=[b_modT[mi]], writes=[b_modT[mi]])

        def ada_gate(m, targets):
            for qd in range(4):
                i = next_wslot()
                c0 = m * D + qd * 512
                P.dma("pool", C.s_wring[i], C.wring[i],
                      w_ada[:, c0:c0 + 512].rearrange("(kc p) n -> p kc n", p=128), writes=[C.b_wring[i]])
                P.dma("pool", C.s_brow, C.brow[0:1, :], b_ada[0:1, c0:c0 + 512], writes=[C.b_brow])
                for ti, (which, gi, scale) in enumerate(targets):
                    bank = 2 + ti
                    for kc in range(16):
                        P.op("pe", "matmul",
                            PS(bank), screp[:, which, kc, :], C.wring[i][:, kc, :], start=(kc == 0), stop=False,
                            reads=[C.b_wring[i], b_sc], writes=[pbank[bank]])
                    P.op("pe", "matmul", PS(bank), ones_b[0:1, :], C.brow[0:1, :],
                                                               start=False, stop=True,
                         reads=[C.b_brow, b_ident], writes=[pbank[bank]])
                    P.op("act", "activation",
                        out=C.G[gi][:, qd * 512:(qd + 1) * 512], in_=PS(bank), func=AF.Copy, scale=scale,
                        reads=[pbank[bank]], writes=[C.b_G[gi]])

        def load_x(row0, nt):
            for t in range(nt):
                P.dma("sp", C.s_X[t], C.X[:, t], x_all[row0 + t * 128: row0 + (t + 1) * 128, :],
                      writes=[C.b_X[t]])

        def norm_T(nt, mi, which, nT, b_nT, src_bufs=None):
            xis = {}

            def stage1(t):
                xi = C.xn_i % 2
                C.xn_i += 1
                xis[t] = xi
                bx = C.b_X[t]
                sv = C.ss[:, 4 * xi:4 * xi + 4]
                bs = C.b_ss2[xi]
                P.op("act", "activation", out=C.xn[xi], in_=C.X[:, t], func=AF.Square, accum_out=sv[:, 0:1],
                     reads=[bx], writes=[bs, C.b_xn[xi]])
                P.op("act", "activation", out=sv[:, 1:2], in_=sv[:, 0:1], func=AF.Sqrt, scale=1.0 / D, bias=EPS,
                     reads=[bs], writes=[bs])
                P.op("dve", "reciprocal", out=sv[:, 2:3], in_=sv[:, 1:2], reads=[bs], writes=[bs])
                P.op("dve", "tensor_scalar", out=C.xn[xi], in0=C.X[:, t], scalar1=sv[:, 2:3], scalar2=None,
                     op0=ALU.mult, reads=[bx, bs], writes=[C.b_xn[xi]])

            def stage2(t):
                xi = xis[t]
                bk = 4 + 2 * (t % 2)
                for kc in range(16):
                    b_ = bk + kc // 8
                    P.op("pe", "transpose",
                         PSB(b_)[:, (kc % 8) * 128:(kc % 8 + 1) * 128], C.xn[xi][:, kc * 128:(kc + 1) * 128], ident,
                         reads=[C.b_xn[xi], b_ident], writes=[pbank[b_]])
                for kc in range(16):
                    b_ = bk + kc // 8
                    P.op("act", "activation",
                         out=nT[:, kc, t * 128:(t + 1) * 128], in_=PSB(b_)[:, (kc % 8) * 128:(kc % 8 + 1) * 128],
                         func=AF.Identity, scale=modT[:, mi * 2 + 1, kc, which:which + 1],
                         bias=modT[:, mi * 2, kc, which:which + 1],
                         reads=[pbank[b_], b_modT[mi * 2], b_modT[mi * 2 + 1]], writes=[b_nT])

            stage1(0)
            if nt > 1:
                stage1(1)
            for t in range(nt):
                stage2(t)
                if t + 2 < nt:
                    stage1(t + 2)

        def ffn(nt, nT, b_nT, wi_b, b_wi, wo_b, b_wo, gi):
            ntok = nt * 128
            for jp in range(NF // 2):
                i = next_wslot()
                P.dma("sp", C.s_wring[i], C.wring[i][:, :, 0:256],
                      wi_b[:, jp * 256:(jp + 1) * 256].rearrange("(kc p) n -> p kc n", p=128),
                      reads=[b_wi], writes=[C.b_wring[i]])
                P.dma("sp", C.s_wring[i], C.wring[i][:, :, 256:512],
                      wi_b[:, DFF + jp * 256: DFF + (jp + 1) * 256].rearrange("(kc p) n -> p kc n", p=128),
                      reads=[b_wi], writes=[C.b_wring[i]])
                for jj in range(2):
                    f = jp * 2 + jj
                    ba, bb = (0, 1) if f % 2 == 0 else (2, 3)
                    for kc in range(16):
                        P.op("pe", "matmul",
                            PS(ba)[:, :ntok], C.wring[i][:, kc, jj * 128:(jj + 1) * 128], nT[:, kc, :ntok],
                            start=(kc == 0), stop=(kc == 15),
                            reads=[C.b_wring[i], b_nT], writes=[pbank[ba]])
                    for kc in range(16):
                        P.op("pe", "matmul",
                            PS(bb)[:, :ntok], C.wring[i][:, kc, 256 + jj * 128:256 + (jj + 1) * 128],
                            nT[:, kc, :ntok], start=(kc == 0), stop=(kc == 15),
                            reads=[C.b_wring[i], b_nT], writes=[pbank[bb]])
                    si = C.silu_i % 2
                    C.silu_i += 1
                    P.op("act", "activation", out=C.silu[si][:, :ntok], in_=PS(ba)[:, :ntok],
                                                                       func=AF.Silu,
                         reads=[pbank[ba]], writes=[C.b_silu[si]])
                    P.op("dve", "tensor_tensor",
                        out=C.gT[:, f, :ntok], in0=C.silu[si][:, :ntok], in1=PS(bb)[:, :ntok], op=ALU.mult,
                        reads=[pbank[bb], C.b_silu[si]], writes=[C.b_gT])
            out_gemm(nt, lambda f, t: C.gT[:, f, t * 128:(t + 1) * 128], [C.b_gT], NF, wo_b, b_wo, gi)

        def out_gemm(nt, lhs_fn, lhs_bufs, nk, wo_b, b_wo, gi):
            for db in range(4):
                for fq in range(nk // 4):
                    i = C.woring_i % 3
                    C.woring_i += 1
                    P.dma("sp", C.s_woring[i], C.woring[i],
                          wo_b[fq * 512:(fq + 1) * 512, db * 512:(db + 1) * 512].rearrange("(f p) n -> p f n", p=128),
                          reads=[b_wo], writes=[C.b_woring[i]])
                    for fi in range(4):
                        f = fq * 4 + fi
                        for t in range(nt):
                            P.op("pe", "matmul",
                                PS(4 + t), lhs_fn(f, t), C.woring[i][:, fi, :], start=(f == 0), stop=(f == nk - 1),
                                reads=[C.b_woring[i]] + lhs_bufs, writes=[pbank[4 + t]])
                for t in range(nt):
                    ti = C.tmpe_i % 2
                    C.tmpe_i += 1
                    P.op("dve", "tensor_tensor",
                        out=C.tmpe[ti], in0=PS(4 + t), in1=C.G[gi][:, db * 512:(db + 1) * 512], op=ALU.mult,
                        reads=[pbank[4 + t], C.b_G[gi]], writes=[C.b_tmpe[ti]])
                    P.op("pool", "tensor_tensor",
                        out=C.X[:, t, db * 512:(db + 1) * 512], in0=C.X[:, t, db * 512:(db + 1) * 512],
                        in1=C.tmpe[ti], op=ALU.add,
                        reads=[C.b_tmpe[ti], C.b_X[t]], writes=[C.b_X[t]])

        alloc_common()
        nT2 = [AR.bf16(16, 512)] * 2
        b_nT2 = [Buf()] * 2
        A_sq = AR.f32(512)
        A_qn = AR.f32(512)
        A_t = [AR.f32(256) for _ in range(4)]
        A_ro = AR.bf16(512)
        A_cs = [AR.f32(128) for _ in range(2)]
        b_cs = [Buf() for _ in range(2)]
        s_cs = [new_slot() for _ in range(2)]
        A_r = AR.f32(16)
        b_qk = Buf()
        gain_t = AR.f32(256)
        b_gain = Buf()
        P.dma("sp", new_slot(), gain_t, gains[:, :], writes=[b_gain])
        NST = 4
        stg = [AR.bf16(512) for _ in range(NST)]
        b_stg = [Buf() for _ in range(NST)]
        s_stg = [new_slot() for _ in range(NST)]
        stg_i = [0]
        s_h1 = [new_slot() for _ in range(4)]
        b_scr = Buf("scratch_all")

        def next_stg():
            i = stg_i[0] % NST
            stg_i[0] += 1
            return i

        gflatA = C.gT.rearrange("p f t -> p (f t)")
        pslots = [(C.wring[k_], C.b_wring[k_], C.s_wring[k_]) for k_ in range(NRING)]
        alias_bufs = []
        for a_ in range(2):
            ab_ = Buf()
            alias_bufs.append(ab_)
            pslots.append((gflatA[:, a_ * 8192:(a_ + 1) * 8192].rearrange("p (k n) -> p k n", k=16), ab_, new_slot()))
        pslot_i = [0]
        rslot_i = [0]

        def handoff(srcs, dsts):
            acc = []
            for b_ in srcs:
                acc += b_.readers + b_.writers + b_.prev
            for d_ in dsts:
                d_.prev = _compress(acc + d_.readers + d_.writers + d_.prev)
                d_.readers = []
                d_.writers = []

        class LazyUnit:
            def __init__(self, c0, rope=False):
                self.c0 = c0
                self.u = None
                self.rope = rope

            def get(self):
                if self.u is None:
                    if self.rope:
                        ap, bf, sl = pslots[2 + rslot_i[0] % 2]
                        rslot_i[0] += 1
                    else:
                        ap, bf, sl = pslots[pslot_i[0] % 2]
                        pslot_i[0] += 1
                    P.dma("sp", sl, ap, wmix_b[:, self.c0:self.c0 + 512].rearrange("(kc p) n -> p kc n", p=128),
                          reads=[b_wmix], writes=[bf])
                    self.u = (ap, bf)
                return self.u

        def tm_gemm(u, t, bank, nT, b_nT):
            ap, bf = u
            for kc in range(16):
                P.op("pe", "matmul", PS(bank), nT[:, kc, t * 128:(t + 1) * 128], ap[:, kc, :],
                     start=(kc == 0), stop=(kc == 15), reads=[bf, b_nT], writes=[pbank[bank]])

        def fm_gemm(u, c4, ntok, bank, nT, b_nT):
            ap, bf = u
            for kc in range(16):
                P.op("pe", "matmul", PS(bank)[:, :ntok], ap[:, kc, c4 * 128:(c4 + 1) * 128],
                     nT[:, kc, :ntok], start=(kc == 0), stop=(kc == 15), reads=[bf, b_nT], writes=[pbank[bank]])

        def evac_store(bank, ncols, func, scale, dst):
            si = next_stg()
            P.op("act", "activation", out=stg[si][:, :ncols], in_=PS(bank)[:, :ncols], func=func, scale=scale,
                 reads=[pbank[bank]], writes=[b_stg[si]])
            P.dma(STQ, s_stg[si], dst, stg[si][:, :ncols], reads=[b_stg[si]], writes=[b_scr])

        def qk_norm_rope(bank, nh, goff, csi):
            w = nh * 128
            pv = PS(bank)[:, :w].rearrange("p (h d) -> p h d", h=nh)
            P.op("act", "activation", out=A_sq[:, :w], in_=PS(bank)[:, :w], func=AF.Square,
                 reads=[pbank[bank]], writes=[b_qk])
            P.op("dve", "tensor_reduce", out=A_r[:, 0:nh], in_=A_sq[:, :w].rearrange("p (h d) -> p h d", h=nh),
                                                  axis=AX.X, op=ALU.add, reads=[b_qk], writes=[b_qk])
            P.op("act", "activation", out=A_r[:, 4:4 + nh], in_=A_r[:, 0:nh], func=AF.Sqrt,
                                               scale=1.0 / 128, bias=EPS, reads=[b_qk], writes=[b_qk])
            P.op("dve", "reciprocal", out=A_r[:, 8:8 + nh], in_=A_r[:, 4:4 + nh], reads=[b_qk], writes=[b_qk])
            qn = A_qn[:, :w].rearrange("p (h d) -> p h d", h=nh)
            P.op("dve", "tensor_tensor", out=qn, in0=pv, in1=bc(A_r[:, 8:8 + nh].unsqueeze(2), [128, nh, 128]),
                                                  op=ALU.mult, reads=[pbank[bank], b_qk], writes=[b_qk])
            P.op("pool", "tensor_tensor", out=qn, in0=qn,
                                                   in1=bc(gain_t[:, goff:goff + 128].unsqueeze(1), [128, nh, 128]),
                                                   op=ALU.mult, reads=[b_qk, b_gain], writes=[b_qk])
            q4 = A_qn[:, :w].rearrange("p (h i two) -> p h i two", h=nh, two=2)
            o4 = A_ro[:, :w].rearrange("p (h i two) -> p h i two", h=nh, two=2)
            cosb = bc(A_cs[csi][:, 0:64].unsqueeze(1), [128, nh, 64])
            sinb = bc(A_cs[csi][:, 64:128].unsqueeze(1), [128, nh, 64])
            tv = [A_t[k][:, :nh * 64].rearrange("p (h i) -> p h i", h=nh) for k in range(4)]
            rd = [b_qk, b_cs[csi]]
            P.op("dve", "tensor_tensor", out=tv[0], in0=q4[:, :, :, 0], in1=cosb, op=ALU.mult, reads=rd, writes=[b_qk])
            P.op("pool", "tensor_tensor", out=tv[1], in0=q4[:, :, :, 1], in1=sinb, op=ALU.mult, reads=rd, writes=[b_qk])
            P.op("pool", "tensor_tensor", out=tv[2], in0=q4[:, :, :, 0], in1=sinb, op=ALU.mult, reads=rd, writes=[b_qk])
            P.op("dve", "tensor_tensor", out=tv[3], in0=q4[:, :, :, 1], in1=cosb, op=ALU.mult, reads=rd, writes=[b_qk])
            P.op("dve", "tensor_tensor", out=o4[:, :, :, 0], in0=tv[0], in1=tv[1], op=ALU.subtract,
                 reads=[b_qk], writes=[b_qk])
            P.op("pool", "tensor_tensor", out=o4[:, :, :, 1], in0=tv[2], in1=tv[3], op=ALU.add,
                 reads=[b_qk], writes=[b_qk])

        def transpose_store(nh, bank, dst):
            for h in range(nh):
                P.op("pe", "transpose", PSB(bank)[:, h * 128:(h + 1) * 128], A_ro[:, h * 128:(h + 1) * 128], ident,
                     reads=[b_qk, b_ident], writes=[pbank[bank]])
            si = next_stg()
            P.op("act", "activation", out=stg[si][:, :nh * 128], in_=PSB(bank)[:, :nh * 128], func=AF.Copy,
                 reads=[pbank[bank]], writes=[b_stg[si]])
            P.dma(STQ, s_stg[si], dst, stg[si][:, :nh * 128].rearrange("p (h t) -> p h t", h=nh),
                  reads=[b_stg[si]], writes=[b_scr])

        KS = 128.0 ** -0.5

        def proj_block(nt, key0, own0, nT, b_nT):
            ntok = nt * 128
            own = own0 is not None
            pb_i = [0]
            rb_i = [0]

            def nbp():
                b_ = pb_i[0] % 4
                pb_i[0] += 1
                return b_

            rope_steps = []
            plain = []

            def add_rope(unit, t, nh, goff, dst, with_v):
                st_ = {}

                def G():
                    k_ = rb_i[0]
                    rb_i[0] += 1
                    csi = k_ % 2
                    bank = 4 + k_ % 2
                    st_["tb"] = 6 + k_ % 2
                    P.dma("sp", s_cs[csi], A_cs[csi], rope[key0 + t * 128: key0 + (t + 1) * 128, :], writes=[b_cs[csi]])
                    tm_gemm(unit.get(), t, bank, nT, b_nT)
                    if with_v:
                        si = next_stg()
                        P.op("act", "activation", out=stg[si][:, :256], in_=PS(bank)[:, 256:512], func=AF.Copy,
                             reads=[pbank[bank]], writes=[b_stg[si]])
                        P.dma(STQ, s_stg[si], v_s[key0 + t * 128: key0 + (t + 1) * 128, :], stg[si][:, :256],
                              reads=[b_stg[si]], writes=[b_scr])
                    qk_norm_rope(bank, nh, goff, csi)

                def F():
                    transpose_store(nh, st_["tb"], dst)
                rope_steps.append((G, F))

            def add_tm(unit, t, func, scale, dst):
                def it():
                    bank = nbp()
                    tm_gemm(unit.get(), t, bank, nT, b_nT)
                    evac_store(bank, 512, func, scale, dst)
                plain.append(it)

            def add_fm(unit, c4, func, scale, dst):
                def it():
                    bank = nbp()
                    fm_gemm(unit.get(), c4, ntok, bank, nT, b_nT)
                    evac_store(bank, ntok, func, scale, dst)
                plain.append(it)

            ukv = LazyUnit(1024, rope=True)
            for t in range(nt):
                add_rope(ukv, t, 2, 128, kT_s[:, :, key0 + t * 128: key0 + (t + 1) * 128].rearrange("h d t -> d h t"), True)
            if own:
                for u in range(2):
                    uq = LazyUnit(u * 512, rope=True)
                    for t in range(nt):
                        add_rope(uq, t, 4, 0, qT_s[u * 4:(u + 1) * 4, :, own0 + t * 128: own0 + (t + 1) * 128]
                                 .rearrange("h d t -> d h t"), False)
                for u in range(2):
                    un = LazyUnit(1536 + u * 512)
                    for c4 in range(4):
                        add_fm(un, c4, AF.Copy, 1.0, qrT_s[u * 4 + c4, :, own0:own0 + ntok])
            for u in range(2):
                un = LazyUnit(2560 + u * 512)
                for t in range(nt):
                    add_tm(un, t, AF.Copy, KS, kr_s[key0 + t * 128: key0 + (t + 1) * 128, u * 512:(u + 1) * 512])
                if own:
                    for c4 in range(4):
                        add_fm(un, c4, AF.Copy, KS, krT_s[u * 4 + c4, :, own0:own0 + ntok])
            for u in range(2):
                un = LazyUnit(3584 + u * 512)
                for t in range(nt):
                    add_tm(un, t, AF.Copy, 1.0, vr_s[key0 + t * 128: key0 + (t + 1) * 128, u * 512:(u + 1) * 512])
            if own:
                for u in range(2):
                    un = LazyUnit(4608 + u * 512)
                    for t in range(nt):
                        add_tm(un, t, AF.Silu, 1.0, sgr_s[own0 + t * 128: own0 + (t + 1) * 128, u * 512:(u + 1) * 512])
                for gsel, dstT in ((0, sgaT_s), (1, sgbT_s)):
                    for u in range(4):
                        un = LazyUnit(5632 + gsel * 2048 + u * 512)
                        for c4 in range(4):
                            add_fm(un, c4, AF.Sigmoid, 1.0, dstT[u * 4 + c4, :, own0:own0 + ntok])
            k_ = -(-len(plain) // len(rope_steps)) if "INTER" in _FL else 0
            pi = 0
            for (G, F) in rope_steps:
                G()
                for it in plain[pi:pi + k_]:
                    it()
                pi += k_
                F()
            for it in plain[pi:]:
                it()

        ada_mod(0, 0, False)
        ada_mod(1, 1, True)
        prepass(w1i, w1i_b, D, b_w1i)
        ada_gate(2, [(1, 0, 0.5), (0, 1, 0.5)])
        prepass(w1o, w1o_b, DFF, b_w1o)
        prepass(wmix, wmix_b, D, b_wmix)
        ada_mod(3, 2, False)
        ada_mod(4, 3, True)

        blocks = [(0, 2, 1, None)]
        for bI in range(8):
            blocks.append((CTX + bI * 512, 4, 0, None))
        for bI in range(8):
            blocks.append((CTX + HALF + bI * 512, 4, 0, bI * 512))
        for bidx, (row0, nt, which, own0) in enumerate(blocks):
            load_x(row0, nt)
            nTa, b_nTa = nT2[0], b_nT2[0]
            norm_T(nt, 0, which, nTa, b_nTa)
            handoff(alias_bufs, [C.b_gT])
            ffn(nt, nTa, b_nTa, w1i_b, b_w1i, w1o_b, b_w1o, 0 if which == 1 else 1)
            handoff([C.b_gT], alias_bufs)
            if own0 is not None:
                for t in range(nt):
                    P.dma("pool", s_h1[t], h1_s[own0 + t * 128: own0 + (t + 1) * 128, :], C.X[:, t],
                          reads=[C.b_X[t]], writes=[b_scr])
            nTb, b_nTb = nT2[1], b_nT2[1]
            norm_T(nt, 1, which, nTb, b_nTb)
            proj_block(nt, row0, own0, nTb, b_nTb)
            if bidx == 2:
                prepass(wpa, wpa_b, 1024, b_wpa)
                prepass(wpr, wpr_b, 1024, b_wpr)
                prepass(wmo, wmo_b, D, b_wmo)
                prepass(w2i, w2i_b, D, b_w2i)
                prepass(w2o, w2o_b, DFF, b_w2o)

        P.barrier()
        AR.reset(persist_off)
        kT_sb = AR.bf16(2, NKEY)
        v_sb = AR.bf16(NKEY // 128, 256)
        b_kv = Buf()
        s_kv = new_slot()
        for h in range(2):
            for c in range(0, NKEY, 2112):
                P.dma("sp", s_kv, kT_sb[:, h, c:c + 2112], kT_s[h, :, c:c + 2112], writes=[b_kv])
        for c in range(0, NKEY // 128, 11):
            P.dma("sp", s_kv, v_sb[:, c:c + 11, :], v_s[c * 128:(c + 11) * 128, :].rearrange("(t p) n -> p t n", p=128),
                  writes=[b_kv])
        qblk = [AR.bf16(512) for _ in range(2)]
        b_qblk = [Buf() for _ in range(2)]
        s_qblk = [new_slot() for _ in range(2)]
        NPT = 6
        PT = [AR.bf16(512) for _ in range(NPT)]
        b_PT = [Buf() for _ in range(NPT)]
        rl = AR.f32(512)
        b_rl = Buf()
        ost = [AR.bf16(512) for _ in range(2)]
        b_ost = [Buf() for _ in range(2)]
        s_ost = [new_slot() for _ in range(2)]
        accD = [AR.f32(512) for _ in range(2)]
        accP = [AR.f32(512) for _ in range(2)]
        b_accD = [Buf() for _ in range(2)]
        b_accP = [Buf() for _ in range(2)]
        ones_f = AR.f32(128)
        b_onesf = Buf()
        P.op("dve", "memset", ones_f, 1.0, writes=[b_onesf])
        NS = NKEY // 128
        SCALE = 128.0 ** -0.5
        qi = 0
        pending = [None]
        for h in range(8):
            kvh = h // 4
            for qb in range(8):
                qs = qi % 2
                P.dma("sp", s_qblk[qs], qblk[qs], qT_s[h, :, qb * 512:(qb + 1) * 512], writes=[b_qblk[qs]])
                ob = 3 + (qi % 2)
                lb = 5 + (qi % 2)

                def mm1(s_):
                    P.op("pe", "matmul", PS(s_ % 3), kT_sb[:, kvh, s_ * 128:(s_ + 1) * 128], qblk[qs],
                         start=True, stop=True, reads=[b_kv, b_qblk[qs]], writes=[pbank[s_ % 3]])
                mm1(0)
                for s_ in range(NS):
                    if s_ + 1 < NS:
                        mm1(s_ + 1)
                    pi = s_ % NPT
                    P.op("act", "activation", out=PT[pi], in_=PS(s_ % 3), func=AF.Exp, scale=SCALE,
                         reads=[pbank[s_ % 3]], writes=[b_PT[pi]])
                    P.op("pe", "matmul", PS(ob), v_sb[:, s_, kvh * 128:(kvh + 1) * 128], PT[pi],
                         start=(s_ == 0), stop=(s_ == NS - 1), reads=[b_kv, b_PT[pi]], writes=[pbank[ob]])
                    if s_ % 3 == 2:
                        eng, acc, bacc, first = "pool", accP[qs], b_accP[qs], (s_ == 2)
                    else:
                        eng, acc, bacc, first = "dve", accD[qs], b_accD[qs], (s_ == 0)
                    if first:
                        P.op(eng, "tensor_copy", out=acc, in_=PT[pi], reads=[b_PT[pi]], writes=[bacc])
                    else:
                        P.op(eng, "tensor_tensor", out=acc, in0=acc, in1=PT[pi], op=ALU.add,
                             reads=[b_PT[pi], bacc], writes=[bacc])
                    if s_ == 3 and pending[0] is not None:
                        pending[0]()
                        pending[0] = None

                def finish(h=h, qb=qb, qs=qs, ob=ob, lb=lb):
                    P.op("pe", "matmul", PS(lb), ones_f, accD[qs], start=True, stop=False,
                         reads=[b_onesf, b_accD[qs]], writes=[pbank[lb]])
                    P.op("pe", "matmul", PS(lb), ones_f, accP[qs], start=False, stop=True,
                         reads=[b_onesf, b_accP[qs]], writes=[pbank[lb]])
                    P.op("dve", "reciprocal", out=rl, in_=PS(lb), reads=[pbank[lb]], writes=[b_rl])
                    P.op("dve", "tensor_tensor", out=ost[qs], in0=PS(ob), in1=rl, op=ALU.mult,
                         reads=[pbank[ob], b_rl], writes=[b_ost[qs]])
                    P.dma("pool", s_ost[qs], yaT_s[h, :, qb * 512:(qb + 1) * 512], ost[qs],
                          reads=[b_ost[qs]], writes=[b_scr])
                pending[0] = finish
                qi += 1
        pending[0]()

        P.barrier()
        AR.reset(persist_off)
        rc = AR.f32(1024)
        b_rc = Buf()
        s_rc = new_slot()
        P.dma("sp", new_slot(), rc, rconst[:, :], writes=[b_rc])
        D1, D2 = rc[:, 0:128], rc[:, 128:256]
        L1, L2 = rc[:, 256:384], rc[:, 384:512]
        I1, I2 = rc[:, 512:640], rc[:, 640:768]
        E_f, E_b = rc[:, 768:800], rc[:, 800:832]
        c127, cp = rc[:, 832:833], rc[:, 833:834]
        lg = AR.f32(16)
        b_lg = Buf()
        P.dma("sp", new_slot(), lg, declg[:, :], writes=[b_lg])
        P.op("act", "activation", out=lg, in_=lg, func=AF.Sigmoid, reads=[b_lg], writes=[b_lg])
        P.op("act", "activation", out=lg, in_=lg, func=AF.Ln, reads=[b_lg], writes=[b_lg])
        tabs = AR.f32(64)
        b_tabs = Buf()
        gf127, gbr, g128 = tabs[:, 0:8], tabs[:, 8:16], tabs[:, 16:32]
        a_f, a_b = tabs[:, 32:40], tabs[:, 40:48]
        P.op("act", "activation", out=gf127, in_=lg[:, 0:8], func=AF.Exp, scale=c127, reads=[b_lg, b_rc], writes=[b_tabs])
        P.op("act", "activation", out=gbr, in_=lg[:, 8:16], func=AF.Exp, scale=cp, reads=[b_lg, b_rc], writes=[b_tabs])
        P.op("act", "activation", out=g128, in_=lg, func=AF.Exp, scale=128.0, reads=[b_lg], writes=[b_tabs])
        P.op("dve", "tensor_scalar", out=tabs[:, 48:49], in0=hf_t, scalar1=4096.0, scalar2=None, op0=ALU.mult,
             reads=[b_hf], writes=[b_tabs])
        P.op("dve", "tensor_scalar", out=tabs[:, 49:50], in0=hf_t, scalar1=-1.0, scalar2=1.0, op0=ALU.mult, op1=ALU.add,
             reads=[b_hf], writes=[b_tabs])
        P.op("dve", "tensor_scalar", out=tabs[:, 50:51], in0=tabs[:, 49:50], scalar1=4096.0, scalar2=None, op0=ALU.mult,
             reads=[b_tabs], writes=[b_tabs])
        omhf = tabs[:, 49:50]
        P.op("act", "activation", out=a_f, in_=lg[:, 0:8], func=AF.Exp, scale=tabs[:, 48:49], reads=[b_lg, b_tabs], writes=[b_tabs])
        P.op("act", "activation", out=a_b, in_=lg[:, 8:16], func=AF.Exp, scale=tabs[:, 50:51], reads=[b_lg, b_tabs], writes=[b_tabs])
        MT = AR.f32(8, 128)
        decF = AR.f32(8, 128)
        decB = AR.f32(8, 128)
        tm1 = AR.f32(128)
        tm2 = AR.f32(128)
        b_tm = Buf()
        b_MT = Buf()
        for h in range(8):
            P.op("act", "activation", out=tm1, in_=D1, func=AF.Exp, scale=lg[:, h:h + 1], reads=[b_lg, b_rc], writes=[b_tm])
            P.op("dve", "tensor_tensor", out=tm1, in0=tm1, in1=L1, op=ALU.mult, reads=[b_tm, b_rc], writes=[b_tm])
            P.op("act", "activation", out=tm2, in_=D2, func=AF.Exp, scale=lg[:, 8 + h:9 + h], reads=[b_lg, b_rc], writes=[b_tm])
            P.op("dve", "tensor_tensor", out=tm2, in0=tm2, in1=L2, op=ALU.mult, reads=[b_tm, b_rc], writes=[b_tm])
            P.op("dve", "tensor_tensor", out=MT[:, h], in0=tm1, in1=tm2, op=ALU.add, reads=[b_tm], writes=[b_MT])
            P.op("act", "activation", out=decF[:, h], in_=I1, func=AF.Exp, scale=lg[:, h:h + 1], reads=[b_lg, b_rc], writes=[b_MT])
            P.op("act", "activation", out=decB[:, h], in_=I2, func=AF.Exp, scale=lg[:, 8 + h:9 + h], reads=[b_lg, b_rc], writes=[b_MT])
        Wt = AR.f32(32, 8)
        Wt2 = AR.f32(32, 8)
        b_Wt = Buf()
        P.op("dve", "tensor_tensor", out=Wt, in0=bc(E_f.unsqueeze(2), [128, 32, 8]), in1=bc(lg[:, 0:8].unsqueeze(1), [128, 32, 8]), op=ALU.mult,
             reads=[b_lg, b_rc], writes=[b_Wt])
        P.op("act", "activation", out=Wt, in_=Wt, func=AF.Exp, reads=[b_Wt], writes=[b_Wt])
        P.op("dve", "tensor_tensor", out=Wt, in0=Wt, in1=bc(gf127.unsqueeze(1), [128, 32, 8]), op=ALU.mult, reads=[b_Wt, b_tabs], writes=[b_Wt])
        P.op("dve", "tensor_scalar", out=Wt, in0=Wt, scalar1=hf_t, scalar2=None, op0=ALU.mult, reads=[b_Wt, b_hf], writes=[b_Wt])
        P.op("dve", "tensor_tensor", out=Wt2, in0=bc(E_b.unsqueeze(2), [128, 32, 8]), in1=bc(lg[:, 8:16].unsqueeze(1), [128, 32, 8]), op=ALU.mult,
             reads=[b_lg, b_rc], writes=[b_Wt])
        P.op("act", "activation", out=Wt2, in_=Wt2, func=AF.Exp, reads=[b_Wt], writes=[b_Wt])
        P.op("dve", "tensor_tensor", out=Wt2, in0=Wt2, in1=bc(gbr.unsqueeze(1), [128, 32, 8]), op=ALU.mult, reads=[b_Wt, b_tabs], writes=[b_Wt])
        P.op("dve", "tensor_scalar", out=Wt2, in0=Wt2, scalar1=omhf, scalar2=None, op0=ALU.mult, reads=[b_Wt, b_tabs], writes=[b_Wt])
        P.op("dve", "tensor_tensor", out=Wt, in0=Wt, in1=Wt2, op=ALU.add, reads=[b_Wt], writes=[b_Wt])
        Wc = AR.f32(4, 8)
        P.op("dve", "tensor_tensor", out=Wc[:, 0], in0=gf127, in1=g128[:, 0:8], op=ALU.mult, reads=[b_tabs], writes=[b_Wt])
        P.op("dve", "tensor_copy", out=Wc[:, 1], in_=gf127, reads=[b_tabs], writes=[b_Wt])
        P.op("dve", "tensor_copy", out=Wc[:, 2], in_=gbr, reads=[b_tabs], writes=[b_Wt])
        P.op("dve", "tensor_tensor", out=Wc[:, 3], in0=gbr, in1=g128[:, 8:16], op=ALU.mult, reads=[b_tabs], writes=[b_Wt])

        NKV = 2
        krb = [AR.bf16(4, 1024) for _ in range(NKV)]
        vrb = [AR.bf16(4, 1024) for _ in range(NKV)]
        b_krb = [Buf() for _ in range(NKV)]
        s_krb = [new_slot() for _ in range(NKV)]
        kvi = [0]

        def load_krvr(row0, nt):
            i = kvi[0] % NKV
            kvi[0] += 1
            P.dma("sp", s_krb[i], krb[i][:, :nt], kr_s[row0:row0 + nt * 128, :].rearrange("(t p) n -> p t n", p=128), writes=[b_krb[i]])
            P.dma("sp", s_krb[i], vrb[i][:, :nt], vr_s[row0:row0 + nt * 128, :].rearrange("(t p) n -> p t n", p=128), writes=[b_krb[i]])
            return i

        krw = [AR.bf16(8, 128) for _ in range(2)]
        b_krw = [Buf() for _ in range(2)]
        krw_i = [0]

        def state_accum(i, tt, wap, banks, first, last, eng="dve"):
            wi = krw_i[0] % 2
            krw_i[0] += 1
            P.op(eng, "tensor_tensor", out=krw[wi], in0=krb[i][:, tt].rearrange("p (h d) -> p h d", h=8),
                                                in1=bc(wap.unsqueeze(2), [128, 8, 128]), op=ALU.mult,
                 reads=[b_krb[i], b_Wt, b_tabs], writes=[b_krw[wi]])
            for h in range(8):
                bk = banks[h // 4]
                P.op("pe", "matmul", PS(bk)[:, (h % 4) * 128:(h % 4 + 1) * 128], krw[wi][:, h, :],
                                                          vrb[i][:, tt, h * 128:(h + 1) * 128], start=first, stop=last,
                     reads=[b_krw[wi], b_krb[i]], writes=[pbank[bk]])

        Sst = AR.f32(8, 128)
        Tst = AR.f32(8, 128)
        b_Sst = Buf()
        b_Tst = Buf()
        accF = AR.f32(8, 128)
        accB = AR.f32(8, 128)
        accO = AR.f32(8, 128)
        b_acc = [Buf() for _ in range(3)]
        bankpairs = [(0, 1), (2, 3), (4, 5), (6, 7)]
        bp_i = [0]

        def accum_tile(i, tt, wap, acc, b_a, firstflag, eng="dve"):
            bp = bankpairs[bp_i[0] % 4]
            bp_i[0] += 1
            state_accum(i, tt, wap, bp, True, True, eng=eng)
            for g in range(2):
                if firstflag:
                    P.op("dve", "tensor_copy", out=acc[:, 4 * g:4 * g + 4], in_=PS(bp[g]).rearrange("p (h d) -> p h d", h=4),
                         reads=[pbank[bp[g]]], writes=[b_a])
                else:
                    P.op("dve", "tensor_tensor", out=acc[:, 4 * g:4 * g + 4], in0=acc[:, 4 * g:4 * g + 4],
                         in1=PS(bp[g]).rearrange("p (h d) -> p h d", h=4), op=ALU.add,
                         reads=[pbank[bp[g]], b_a], writes=[b_a])

        i = load_krvr(0, 2)
        for tt in range(2):
            accum_tile(i, tt, Wc[:, tt], accF, b_acc[0], tt == 0)
            accum_tile(i, tt, Wc[:, 2 + tt], accB, b_acc[1], tt == 0)
        for t4 in range(8):
            i = load_krvr(CTX + t4 * 512, 4)
            for tt in range(4):
                t = t4 * 4 + tt
                accum_tile(i, tt, Wt[:, t], accO, b_acc[2], t == 0, eng=("dve" if tt % 2 == 0 else "pool"))
        P.op("dve", "tensor_tensor", out=Sst, in0=accF, in1=bc(a_f.unsqueeze(2), [128, 8, 128]), op=ALU.mult,
             reads=[b_acc[0], b_tabs], writes=[b_Sst])
        P.op("dve", "tensor_tensor", out=Tst, in0=accB, in1=bc(a_b.unsqueeze(2), [128, 8, 128]), op=ALU.mult,
             reads=[b_acc[1], b_tabs], writes=[b_Tst])
        P.op("dve", "scalar_tensor_tensor", out=Sst, in0=accO, scalar=hf_t, in1=Sst, op0=ALU.mult, op1=ALU.add,
             reads=[b_acc[2], b_hf, b_Sst], writes=[b_Sst])
        P.op("dve", "scalar_tensor_tensor", out=Tst, in0=accO, scalar=omhf, in1=Tst, op0=ALU.mult, op1=ALU.add,
             reads=[b_acc[2], b_tabs, b_Tst], writes=[b_Tst])
        Tb = AR.bf16(32, 1024)
        b_Tb = Buf()
        OWN0 = CTX + HALF
        for c4 in range(7, -1, -1):
            i = load_krvr(OWN0 + c4 * 512, 4)
            for tt in range(3, -1, -1):
                c = c4 * 4 + tt
                P.op("act", "activation", out=Tb[:, c], in_=Tst.rearrange("p h d -> p (h d)"), func=AF.Copy,
                     reads=[b_Tst], writes=[b_Tb])
                if c == 0:
                    break
                state_accum(i, tt, gbr, (6, 7), True, True, eng="pool")
                P.op("pool", "tensor_tensor", out=Tst, in0=Tst, in1=bc(g128[:, 8:16].unsqueeze(2), [128, 8, 128]), op=ALU.mult,
                     reads=[b_Tst, b_tabs], writes=[b_Tst])
                for g in range(2):
                    P.op("dve", "tensor_tensor", out=Tst[:, 4 * g:4 * g + 4], in0=Tst[:, 4 * g:4 * g + 4],
                                                               in1=PS(6 + g).rearrange("p (h d) -> p h d", h=4), op=ALU.add,
                         reads=[pbank[6 + g], b_Tst], writes=[b_Tst])
        qrb = [AR.bf16(8, 512)] * 2
        ktb = [AR.bf16(8, 512)] * 2
        sgb_ = [AR.bf16(4, 1024)] * 2
        b_qrb = [Buf()] * 2
        s_qrb = [new_slot()] * 2
        Sb16 = AR.bf16(8, 128)
        b_Sb16 = Buf()
        qf = AR.bf16(8, 128)
        qbk = AR.bf16(8, 128)
        b_qf = Buf()
        Pm = AR.bf16(8, 128)
        b_Pm = Buf()
        ysq = AR.f32(1024)
        yn = AR.f32(1024)
        yg = AR.bf16(1024)
        b_y = Buf()
        yr_ = AR.f32(32)
        yst = [AR.bf16(1024) for _ in range(2)]
        b_yst = [Buf() for _ in range(2)]
        s_yst = [new_slot() for _ in range(2)]
        for c4 in range(8):
            i = load_krvr(OWN0 + c4 * 512, 4)
            j = c4 % 2
            P.dma("sp", s_qrb[j], qrb[j], qrT_s[:, :, c4 * 512:(c4 + 1) * 512].rearrange("h d t -> d h t"), writes=[b_qrb[j]])
            P.dma("sp", s_qrb[j], ktb[j], krT_s[:, :, c4 * 512:(c4 + 1) * 512].rearrange("h d t -> d h t"), writes=[b_qrb[j]])
            P.dma("sp", s_qrb[j], sgb_[j], sgr_s[c4 * 512:(c4 + 1) * 512, :].rearrange("(t p) n -> p t n", p=128), writes=[b_qrb[j]])
            for tt in range(4):
                c = c4 * 4 + tt
                ts = slice(tt * 128, (tt + 1) * 128)
                P.op("act", "activation", out=Sb16.rearrange("p h d -> p (h d)"), in_=Sst.rearrange("p h d -> p (h d)"), func=AF.Copy,
                     reads=[b_Sst], writes=[b_Sb16])
                P.op("dve", "tensor_tensor", out=qf, in0=qrb[j][:, :, ts], in1=decF, op=ALU.mult,
                     reads=[b_qrb[j], b_MT], writes=[b_qf])
                P.op("pool", "tensor_tensor", out=qbk, in0=qrb[j][:, :, ts], in1=decB, op=ALU.mult,
                     reads=[b_qrb[j], b_MT], writes=[b_qf])
                for h in range(8):
                    bk = h // 4
                    P.op("pe", "matmul", PS(bk)[:, (h % 4) * 128:(h % 4 + 1) * 128], ktb[j][:, h, ts],
                                                                    qrb[j][:, h, ts], start=True, stop=True,
                         reads=[b_qrb[j]], writes=[pbank[bk]])
                for g in range(2):
                    P.op("dve", "tensor_tensor", out=Pm[:, 4 * g:4 * g + 4], in0=PS(g).rearrange("p (h d) -> p h d", h=4),
                                                               in1=MT[:, 4 * g:4 * g + 4], op=ALU.mult,
                         reads=[pbank[g], b_MT], writes=[b_Pm])
                for h in range(8):
                    bk = 2 + h // 4
                    osl = slice((h % 4) * 128, (h % 4 + 1) * 128)
                    P.op("pe", "matmul", PS(bk)[:, osl], Pm[:, h, :], vrb[i][:, tt, h * 128:(h + 1) * 128],
                                                                      start=True, stop=False,
                         reads=[b_Pm, b_krb[i]], writes=[pbank[bk]])
                    P.op("pe", "matmul", PS(bk)[:, osl], qf[:, h, :], Sb16[:, h, :], start=False, stop=False,
                         reads=[b_qf, b_Sb16], writes=[pbank[bk]])
                    P.op("pe", "matmul", PS(bk)[:, osl], qbk[:, h, :], Tb[:, c, h * 128:(h + 1) * 128],
                                                                           start=False, stop=True,
                         reads=[b_qf, b_Tb], writes=[pbank[bk]])
                for g in range(2):
                    P.op("act", "activation", out=ysq[:, g * 512:(g + 1) * 512], in_=PS(2 + g), func=AF.Square,
                         reads=[pbank[2 + g]], writes=[b_y])
                P.op("dve", "tensor_reduce", out=yr_[:, 0:8], in_=ysq.rearrange("p (h d) -> p h d", h=8), axis=AX.X, op=ALU.add,
                     reads=[b_y], writes=[b_y])
                P.op("act", "activation", out=yr_[:, 8:16], in_=yr_[:, 0:8], func=AF.Sqrt, scale=1.0 / 128, bias=EPS,
                     reads=[b_y], writes=[b_y])
                P.op("dve", "reciprocal", out=yr_[:, 16:24], in_=yr_[:, 8:16], reads=[b_y], writes=[b_y])
                for g in range(2):
                    P.op("dve", "tensor_tensor", out=yn[:, g * 512:(g + 1) * 512].rearrange("p (h d) -> p h d", h=4),
                                                               in0=PS(2 + g).rearrange("p (h d) -> p h d", h=4),
                                                               in1=bc(yr_[:, 16 + 4 * g:20 + 4 * g].unsqueeze(2), [128, 4, 128]), op=ALU.mult,
                         reads=[pbank[2 + g], b_y], writes=[b_y])
                P.op("pool", "tensor_tensor", out=yg, in0=yn, in1=sgb_[j][:, tt], op=ALU.mult,
                     reads=[b_y, b_qrb[j]], writes=[b_y])
                for h in range(8):
                    P.op("pe", "transpose", PSB(4)[:, h * 128:(h + 1) * 128], yg[:, h * 128:(h + 1) * 128], ident,
                         reads=[b_y, b_ident], writes=[pbank[4]])
                yi = c % 2
                P.op("act", "activation", out=yst[yi], in_=PSB(4), func=AF.Copy, reads=[pbank[4]], writes=[b_yst[yi]])
                P.dma(STQ, s_yst[yi], yrT_s[:, :, c * 128:(c + 1) * 128].rearrange("h e t -> e h t"),
                      yst[yi].rearrange("p (h t) -> p h t", h=8), reads=[b_yst[yi]], writes=[b_scr])
                if c < 31:
                    state_accum(i, tt, gf127, (6, 7), True, True, eng="pool")
                    P.op("pool", "tensor_tensor", out=Sst, in0=Sst, in1=bc(g128[:, 0:8].unsqueeze(2), [128, 8, 128]), op=ALU.mult,
                         reads=[b_Sst, b_tabs, b_Sb16], writes=[b_Sst])
                    for g in range(2):
                        P.op("dve", "tensor_tensor", out=Sst[:, 4 * g:4 * g + 4], in0=Sst[:, 4 * g:4 * g + 4],
                                                                   in1=PS(6 + g).rearrange("p (h d) -> p h d", h=4), op=ALU.add,
                             reads=[pbank[6 + g], b_Sst], writes=[b_Sst])

        P.barrier()
        AR.reset(persist_off)
        alloc_common()
        nT3 = AR.bf16(16, 512)
        b_nT3 = Buf()
        uT = nT3
        b_uT = b_nT3
        yrb = AR.bf16(8, 512)
        b_yrb = Buf()
        s_yrb = new_slot()
        fn_t = AR.f32(D)
        b_fn = Buf()
        P.dma("sp", new_slot(), fn_t, fnorm[:, :], writes=[b_fn])
        mt1 = AR.f32(512)
        mt2 = AR.f32(512)
        b_mt = Buf()
        s_gt = new_slot()
        s_out = [new_slot() for _ in range(4)]
        gflat = C.gT.rearrange("p f t -> p (f t)")
        sga = gflat[:, 0:8192].rearrange("p (k t) -> p k t", k=16)
        sgbb = gflat[:, 8192:16384].rearrange("p (k t) -> p k t", k=16)
        yab = gflat[:, 16384:20480].rearrange("p (k t) -> p k t", k=8)
        ada_gate(5, [(0, 0, 1.0)])
        ada_mod(6, 4, False)
        ada_mod(7, 5, True)
        ada_gate(8, [(0, 1, 0.5)])
        for bI in range(8):
            t0 = bI * 512
            for t in range(4):
                P.dma("sp", C.s_X[t], C.X[:, t], h1_s[t0 + t * 128: t0 + (t + 1) * 128, :], reads=[b_scr], writes=[C.b_X[t]])
            P.dma("sp", s_gt, sga, sgaT_s[:, :, t0:t0 + 512].rearrange("k d t -> d k t"), reads=[b_scr], writes=[C.b_gT])
            P.dma("sp", s_gt, sgbb, sgbT_s[:, :, t0:t0 + 512].rearrange("k d t -> d k t"), reads=[b_scr], writes=[C.b_gT])
            P.dma("sp", s_gt, yab, yaT_s[:, :, t0:t0 + 512].rearrange("k d t -> d k t"), reads=[b_scr], writes=[C.b_gT])
            P.dma("sp", s_yrb, yrb, yrT_s[:, :, t0:t0 + 512].rearrange("k d t -> d k t"), reads=[b_scr], writes=[b_yrb])
            for u in range(4):
                i = next_wslot()
                P.dma("sp", C.s_wring[i], C.wring[i][:, 0:8, :], wpa_b[:, u * 512:(u + 1) * 512].rearrange("(kc p) n -> p kc n", p=128),
                      reads=[b_wpa], writes=[C.b_wring[i]])
                P.dma("sp", C.s_wring[i], C.wring[i][:, 8:16, :], wpr_b[:, u * 512:(u + 1) * 512].rearrange("(kc p) n -> p kc n", p=128),
                      reads=[b_wpr], writes=[C.b_wring[i]])
                for c4 in range(4):
                    dmc = u * 4 + c4
                    ba, bb = (0, 1) if dmc % 2 == 0 else (2, 3)
                    for kc in range(8):
                        P.op("pe", "matmul", PS(ba), C.wring[i][:, kc, c4 * 128:(c4 + 1) * 128],
                                                                              yab[:, kc, :], start=(kc == 0), stop=(kc == 7),
                             reads=[C.b_wring[i], C.b_gT], writes=[pbank[ba]])
                    for kc in range(8):
                        P.op("pe", "matmul", PS(bb), C.wring[i][:, 8 + kc, c4 * 128:(c4 + 1) * 128],
                                                                              yrb[:, kc, :], start=(kc == 0), stop=(kc == 7),
                             reads=[C.b_wring[i], b_yrb], writes=[pbank[bb]])
                    P.op("dve", "tensor_tensor", out=mt1, in0=PS(ba), in1=sga[:, dmc, :], op=ALU.mult,
                         reads=[pbank[ba], C.b_gT], writes=[b_mt])
                    P.op("dve", "tensor_tensor", out=mt2, in0=PS(bb), in1=sgbb[:, dmc, :], op=ALU.mult,
                         reads=[pbank[bb], C.b_gT], writes=[b_mt])
                    P.op("pool", "tensor_tensor", out=uT[:, dmc, :], in0=mt1, in1=mt2, op=ALU.add,
                         reads=[b_mt], writes=[b_uT])
            out_gemm(4, lambda k, t: uT[:, k, t * 128:(t + 1) * 128], [b_uT], 16, wmo_b, b_wmo, 0)
            norm_T(4, 2, 0, nT3, b_nT3)
            ffn(4, nT3, b_nT3, w2i_b, b_w2i, w2o_b, b_w2o, 1)
            for t in range(4):
                P.op("act", "activation", out=C.xn[0], in_=C.X[:, t], func=AF.Square, accum_out=C.ss[:, 0:1],
                     reads=[C.b_X[t]], writes=[C.b_ss, C.b_xn[0]])
                P.op("act", "activation", out=C.ss[:, 1:2], in_=C.ss[:, 0:1], func=AF.Sqrt, scale=1.0 / D, bias=EPS,
                     reads=[C.b_ss], writes=[C.b_ss])
                P.op("dve", "reciprocal", out=C.ss[:, 2:3], in_=C.ss[:, 1:2], reads=[C.b_ss], writes=[C.b_ss])
                P.op("dve", "scalar_tensor_tensor", out=C.X[:, t], in0=C.X[:, t], scalar=C.ss[:, 2:3], in1=fn_t,
                                                                  op0=ALU.mult, op1=ALU.mult,
                     reads=[C.b_X[t], C.b_ss, b_fn], writes=[C.b_X[t]])
                P.dma("pool", s_out[t], out[t0 + t * 128: t0 + (t + 1) * 128, :], C.X[:, t], reads=[C.b_X[t]], writes=[b_scr])
        P.barrier()
        P.finalize(block, engsem)
    return nc


def _rope_tables():
    rows = SEQ // 64
    row = np.repeat(np.arange(rows, dtype=np.float32), 64)
    col = np.tile(np.arange(64, dtype=np.float32), rows)
    half = 64
    inv_freq = (10000.0 ** (-np.arange(0, half, 2, dtype=np.float32) / half)).astype(np.float32)
    ang = np.concatenate([row[:, None] * inv_freq, col[:, None] * inv_freq], axis=-1).astype(np.float32)
    return np.cos(ang).astype(np.float32), np.sin(ang).astype(np.float32)


def _rconst():
    rcn = np.zeros((128, 1024), np.float32)
    p = np.arange(128, dtype=np.float32)
    j = p[:, None]
    i = p[None, :]
    rcn[:, 0:128] = np.maximum(i - j, 0)
    rcn[:, 128:256] = np.maximum(j - i, 0)
    rcn[:, 256:384] = (i >= j)
    rcn[:, 384:512] = (j >= i)
    rcn[:, 512:640] = i + 1
    rcn[:, 640:768] = 128 - i
    t = np.arange(32, dtype=np.float32)
    rcn[:, 768:800] = 128 * (31 - t)[None, :]
    rcn[:, 800:832] = 128 * t[None, :]
    rcn[:, 832] = 127 - p
    rcn[:, 833] = p
    return rcn


_NC_CACHE = {}


def kernel(x, c, ctx, c_ctx, w_ada, b_ada, ffn1_w_in, ffn1_w_out, mix_w_in, attn_q_gain, attn_k_gain,
           ret_decay_logit, w_proj_attn, w_proj_ret, mix_w_out, ffn2_w_in, ffn2_w_out, final_norm):
    f = lambda a: np.ascontiguousarray(np.asarray(a, dtype=np.float32))
    x, c, ctx, c_ctx = f(x), f(c), f(ctx), f(c_ctx)
    if "nc" not in _NC_CACHE:
        _NC_CACHE["nc"] = build_program()
    nc = _NC_CACHE["nc"]
    cos, sin = _rope_tables()
    rope_x = np.concatenate([cos, sin], axis=1)
    rope_ctx = np.concatenate([np.ones((CTX, 64), np.float32), np.zeros((CTX, 64), np.float32)], axis=1)
    shared = {
        "b_adaT": f(np.asarray(b_ada)[0].reshape(144, 128).T),
        "b_ada": f(np.asarray(b_ada)[0][None, :]),
        "w_ada": f(np.asarray(w_ada)[0]),
        "ffn1_w_in": f(np.asarray(ffn1_w_in)[0]),
        "ffn1_w_out": f(np.asarray(ffn1_w_out)[0]),
        "mix_w_in": f(np.asarray(mix_w_in)[0]),
        "gains": f(np.tile(np.concatenate([np.asarray(attn_q_gain)[0], np.asarray(attn_k_gain)[0]])[None, :], (128, 1))),
        "declg": f(np.tile(np.asarray(ret_decay_logit)[0].reshape(1, 16), (128, 1))),
        "w_proj_attn": f(np.asarray(w_proj_attn)[0]),
        "w_proj_ret": f(np.asarray(w_proj_ret)[0]),
        "mix_w_out": f(np.asarray(mix_w_out)[0]),
        "ffn2_w_in": f(np.asarray(ffn2_w_in)[0]),
        "ffn2_w_out": f(np.asarray(ffn2_w_out)[0]),
        "fnorm": f(np.tile(np.asarray(final_norm)[None, :], (128, 1))),
        "rconst": _rconst(),
    }
    in_maps = []
    for core in range(8):
        b, hf = core // 2, core % 2
        own = slice(hf * HALF, (hf + 1) * HALF)
        oth = slice((1 - hf) * HALF, (2 - hf) * HALF)
        m = dict(shared)
        m["x_all"] = np.concatenate([ctx[b], x[b, oth], x[b, own]], axis=0)
        m["rope"] = np.concatenate([rope_ctx, rope_x[oth], rope_x[own]], axis=0)
        cc = np.stack([c[b].reshape(16, 128).T, c_ctx.reshape(16, 128).T], axis=-1)
        m["cT"] = f(cc.reshape(128, 32))
        m["hf"] = np.full((128, 1), float(hf), np.float32)
        in_maps.append(m)
    res = run_bass_kernel_spmd(nc, in_maps, core_ids=list(range(8)))
    outp = np.empty((4, SEQ, D), np.float32)
    for core in range(8):
        b, hf = core // 2, core % 2
        outp[b, hf * HALF:(hf + 1) * HALF] = res.results[core]["out"]
    return outp
```
